# Optimizing a Trainium2 kernel written in Bass

```python
import math
import jax, jax.numpy as jnp
from jax import lax
import numpy as np

D_MODEL = 1024
BATCH = 8
SEQ = 2048
DEPTH = 4

GRID_W = 64
CTX_LEN = 256
EPS = 1e-6
D_RNN = 1024
LRU_BLOCKS = 16
LRU_BW = D_RNN // LRU_BLOCKS
CONV_W = 4
CONV_LEFT = 2
LRU_C = 8.0
ATTN_HEADS = 8
ATTN_DH = D_MODEL // (2 * ATTN_HEADS)
D_QK = ATTN_HEADS * 2 * ATTN_DH
D_V = ATTN_HEADS * 2 * ATTN_DH
Q_BLOCK = 128
ROPE_THETA = 10000.0
ROPE_FREQS = ATTN_DH // 4
PEER_HEADS = 8
N_KEYS = 128
N_EXPERTS = N_KEYS * N_KEYS
PEER_DQ = 128
PEER_TOPK = 16
PEER_CHUNK = 32
D_IN = 2 * D_RNN + 2 * D_QK + D_V + 2 * D_MODEL
SPLIT_AT = (D_RNN, 2 * D_RNN, 2 * D_RNN + D_QK, 2 * D_RNN + 2 * D_QK,
            2 * D_RNN + 2 * D_QK + D_V, 2 * D_RNN + 2 * D_QK + D_V + D_MODEL)

kernel_name = 'hybrid_rglru_diffattn_peer_dit'


def rmsnorm(x, g):
    xf = x.astype(jnp.float32)
    y = xf * lax.rsqrt(jnp.mean(xf * xf, axis=-1, keepdims=True) + EPS)
    return (y * g.astype(jnp.float32)).astype(x.dtype)


def modulate(h, shift, scale):
    return h * (1.0 + scale) + shift


def dwconv_centred(x, w, b):
    L = x.shape[1]
    xp = jnp.pad(x, ((0, 0), (CONV_LEFT, CONV_W - 1 - CONV_LEFT), (0, 0)))
    y = b
    for k in range(CONV_W):
        y = y + xp[:, k:k + L] * w[k]
    return y


def block_diag_linear(x, w, b):
    B, L, _ = x.shape
    xb = x.reshape(B, L, LRU_BLOCKS, LRU_BW)
    return jnp.einsum('blni,nij->blnj', xb, w).reshape(B, L, D_RNN) + b


def rglru_scan(u, w, b, lam, h0, reverse):
    uf = u.astype(jnp.float32)
    rec = jax.nn.sigmoid(block_diag_linear(u, w[0], b[0]).astype(jnp.float32))
    inp = jax.nn.sigmoid(block_diag_linear(u, w[1], b[1]).astype(jnp.float32))
    log_a = -LRU_C * rec * jax.nn.softplus(-lam.astype(jnp.float32))
    a = jnp.exp(log_a)
    drive = jnp.sqrt(-jnp.expm1(2.0 * log_a)) * (inp * uf)
    edge = -1 if reverse else 0
    drive = drive.at[:, edge].add(a[:, edge] * h0)

    def combine(earlier, later):
        return (earlier[0] * later[0], later[0] * earlier[1] + later[1])

    _, h = lax.associative_scan(combine, (a, drive), reverse=reverse, axis=1)
    return h, (h[:, 0] if reverse else h[:, -1])


def axial_rope_tables(n_tokens):
    rows = n_tokens // GRID_W
    row, col = jnp.meshgrid(jnp.arange(rows), jnp.arange(GRID_W), indexing='ij')
    pos = jnp.stack([row.reshape(-1), col.reshape(-1)], axis=-1).astype(jnp.float32)
    inv = ROPE_THETA ** (-jnp.arange(ROPE_FREQS, dtype=jnp.float32) / ROPE_FREQS)
    ang = pos[:, :, None] * inv
    return jnp.cos(ang), jnp.sin(ang)


def apply_rope(x, cos, sin):
    B, L, H, M, _ = x.shape
    xr = x.astype(jnp.float32).reshape(B, L, H, M, 2, 2, ROPE_FREQS)
    x1, x2 = xr[..., 0, :], xr[..., 1, :]
    c = cos[None, :, None, None]
    s = sin[None, :, None, None]
    out = jnp.stack([x1 * c - x2 * s, x2 * c + x1 * s], axis=-2)
    return out.reshape(x.shape).astype(x.dtype)


def diff_attend(q, k, v, lam):
    B, Lq = q.shape[:2]
    nb = Lq // Q_BLOCK
    qb = q.reshape(B, nb, Q_BLOCK, ATTN_HEADS, 2, ATTN_DH).swapaxes(0, 1)
    vf = v.astype(jnp.float32)

    def one_block(qi):
        s = jnp.einsum('bqhmd,bkhmd->bhmqk', qi, k, preferred_element_type=jnp.float32)
        p = jax.nn.softmax(s, axis=-1)
        w = p[:, :, 0] - lam * p[:, :, 1]
        return jnp.einsum('bhqk,bkhe->bqhe', w, vf)

    o = lax.map(one_block, qb)
    return o.swapaxes(0, 1).reshape(B, Lq, ATTN_HEADS, 2 * ATTN_DH)


def mixer(hx, hc, w_in, conv_w, conv_b, lru_w, lru_b, lru_lam, diff_lam, subln_g,
          w_br_lru, w_br_attn, w_out, lambda_init, cos, sin, update_ctx):
    B, L, _ = hx.shape
    C = hc.shape[1]
    dt = hx.dtype
    ul, gl, ql, kl, vl, gal, gbl = jnp.split(hx @ w_in, SPLIT_AT, axis=-1)
    uc, gc, qc, kc, vc, gac, gbc = jnp.split(hc @ w_in, SPLIT_AT, axis=-1)

    uc = dwconv_centred(uc, conv_w, conv_b)
    ul = dwconv_centred(ul, conv_w, conv_b)
    h0 = jnp.zeros((B, D_RNN), jnp.float32)
    hsum_c = jnp.zeros((B, C, D_RNN), jnp.float32)
    hsum_l = jnp.zeros((B, L, D_RNN), jnp.float32)
    for d, rev in enumerate((False, True)):
        h_c, state_c = rglru_scan(uc, lru_w[d], lru_b[d], lru_lam[d], h0, rev)
        h_l, _ = rglru_scan(ul, lru_w[d], lru_b[d], lru_lam[d], state_c, rev)
        hsum_c = hsum_c + h_c
        hsum_l = hsum_l + h_l

    def lru_branch(hsum, gate):
        return (hsum * jax.nn.gelu(gate.astype(jnp.float32))).astype(dt) @ w_br_lru

    lp = diff_lam.astype(jnp.float32)
    lam = jnp.exp(jnp.sum(lp[0] * lp[1])) - jnp.exp(jnp.sum(lp[2] * lp[3])) + lambda_init
    scale = ATTN_DH ** -0.5
    ql = apply_rope(ql.reshape(B, L, ATTN_HEADS, 2, ATTN_DH), cos, sin) * scale
    kl = apply_rope(kl.reshape(B, L, ATTN_HEADS, 2, ATTN_DH), cos, sin)
    qc = qc.reshape(B, C, ATTN_HEADS, 2, ATTN_DH) * scale
    kc = kc.reshape(B, C, ATTN_HEADS, 2, ATTN_DH)
    vl = vl.reshape(B, L, ATTN_HEADS, 2 * ATTN_DH)
    vc = vc.reshape(B, C, ATTN_HEADS, 2 * ATTN_DH)

    def attn_branch(q, k, v):
        o = rmsnorm(diff_attend(q, k, v, lam), subln_g) * (1.0 - lambda_init)
        return o.reshape(q.shape[0], q.shape[1], D_V).astype(dt) @ w_br_attn

    ya_l = attn_branch(ql, jnp.concatenate([kc, kl], axis=1), jnp.concatenate([vc, vl], axis=1))
    out_l = (jax.nn.sigmoid(gal) * lru_branch(hsum_l, gl) + jax.nn.sigmoid(gbl) * ya_l) @ w_out
    out_c = None
    if update_ctx:
        ya_c = attn_branch(qc, kc, vc)
        out_c = (jax.nn.sigmoid(gac) * lru_branch(hsum_c, gc) + jax.nn.sigmoid(gbc) * ya_c) @ w_out
    return out_l, out_c


def peer_ffn(h, wq, keys, u_tab, v_tab):
    B, L, D = h.shape
    q = (h @ wq).reshape(B, L, PEER_HEADS, 2, PEER_DQ // 2).astype(jnp.float32)
    s = jnp.einsum('blhpd,hpnd->blhpn', q, keys.astype(jnp.float32))
    v_top, i_top = lax.top_k(s, PEER_TOPK)
    cand = v_top[..., 0, :, None] + v_top[..., 1, None, :]
    best, pos = lax.top_k(cand.reshape(B, L, PEER_HEADS, PEER_TOPK * PEER_TOPK), PEER_TOPK)
    i1 = jnp.take_along_axis(i_top[..., 0, :], pos // PEER_TOPK, axis=-1)
    i2 = jnp.take_along_axis(i_top[..., 1, :], pos % PEER_TOPK, axis=-1)
    idx = i1 * N_KEYS + i2
    g = jax.nn.softmax(best, axis=-1)
    nc = L // PEER_CHUNK

    def chunk(a):
        return a.reshape((B, nc, PEER_CHUNK) + a.shape[2:]).swapaxes(0, 1)

    def body(args):
        hc, ic, gc = args
        act = jax.nn.gelu(jnp.einsum('btd,bthkd->bthk', hc, u_tab[ic]).astype(jnp.float32))
        return jnp.einsum('bthk,bthkd->btd', (act * gc).astype(h.dtype), v_tab[ic])

    y = lax.map(body, (chunk(h), chunk(idx), chunk(g)))
    return y.swapaxes(0, 1).reshape(B, L, D)


def setup_inputs(seed: int = 0) -> dict:
    key = jax.random.key(seed)
    ks = jax.random.split(key, 24)

    def nrm(k, shape, s):
        return jax.random.normal(k, shape, jnp.float32) * s

    D = D_MODEL
    u = jax.random.uniform(ks[13], (DEPTH, 2, D_RNN), jnp.float32, minval=0.9, maxval=0.999)
    a = u ** (1.0 / LRU_C)
    return {
        'x': nrm(ks[0], (BATCH, SEQ, D), 1.0),
        'c': nrm(ks[1], (BATCH, D), 1.0),
        'ctx': nrm(ks[2], (BATCH, CTX_LEN, D), 1.0),
        'c_ctx': nrm(ks[3], (D,), 1.0),
        'mod_w': nrm(ks[4], (DEPTH, D, 6 * D), 0.5 * D ** -0.5),
        'mod_b': nrm(ks[5], (DEPTH, 6 * D), 0.01),
        'norm1_g': 1.0 + nrm(ks[6], (DEPTH, D), 0.01),
        'norm2_g': 1.0 + nrm(ks[7], (DEPTH, D), 0.01),
        'w_in': nrm(ks[8], (DEPTH, D, D_IN), D ** -0.5),
        'conv_w': nrm(ks[9], (DEPTH, CONV_W, D_RNN), CONV_W ** -0.5),
        'conv_b': nrm(ks[10], (DEPTH, D_RNN), 0.01),
        'lru_w': nrm(ks[11], (DEPTH, 2, 2, LRU_BLOCKS, LRU_BW, LRU_BW), LRU_BW ** -0.5),
        'lru_b': nrm(ks[12], (DEPTH, 2, 2, D_RNN), 0.01),
        'lru_lam': jnp.log(a) - jnp.log1p(-a),
        'diff_lam': nrm(ks[14], (DEPTH, 4, ATTN_DH), 0.1),
        'subln_g': 1.0 + nrm(ks[15], (DEPTH, 2 * ATTN_DH), 0.01),
        'w_br_lru': nrm(ks[16], (DEPTH, D_RNN, D), D_RNN ** -0.5),
        'w_br_attn': nrm(ks[17], (DEPTH, D_V, D), D_V ** -0.5),
        'w_out': nrm(ks[18], (DEPTH, D, D), D ** -0.5),
        'peer_wq': nrm(ks[19], (DEPTH, D, PEER_HEADS * PEER_DQ), D ** -0.5),
        'peer_keys': nrm(ks[20], (DEPTH, PEER_HEADS, 2, N_KEYS, PEER_DQ // 2), (PEER_DQ // 2) ** -0.5),
        'peer_u': nrm(ks[21], (DEPTH, N_EXPERTS, D), D ** -0.5),
        'peer_v': nrm(ks[22], (DEPTH, N_EXPERTS, D), PEER_HEADS ** -0.5),
        'final_g': 1.0 + nrm(ks[23], (D,), 0.01),
    }


def reference(x, c, ctx, c_ctx, mod_w, mod_b, norm1_g, norm2_g, w_in, conv_w, conv_b,
              lru_w, lru_b, lru_lam, diff_lam, subln_g, w_br_lru, w_br_attn, w_out,
              peer_wq, peer_keys, peer_u, peer_v, final_g):
    L = x.shape[1]
    cos, sin = axial_rope_tables(L)
    s_lat = jax.nn.silu(c)
    s_ctx = jax.nn.silu(c_ctx)
    for li in range(DEPTH):
        update_ctx = li < DEPTH - 1
        lambda_init = 0.8 - 0.6 * math.exp(-0.3 * li)
        m_l = jnp.split((s_lat @ mod_w[li] + mod_b[li])[:, None, :], 6, axis=-1)
        m_c = jnp.split(s_ctx @ mod_w[li] + mod_b[li], 6, axis=-1)
        hx = modulate(rmsnorm(x, norm1_g[li]), m_l[0], m_l[1])
        hc = modulate(rmsnorm(ctx, norm1_g[li]), m_c[0], m_c[1])
        out_l, out_c = mixer(hx, hc, w_in[li], conv_w[li], conv_b[li], lru_w[li], lru_b[li],
                             lru_lam[li], diff_lam[li], subln_g[li], w_br_lru[li], w_br_attn[li],
                             w_out[li], lambda_init, cos, sin, update_ctx)
        x = x + m_l[2] * out_l
        x = x + m_l[5] * peer_ffn(modulate(rmsnorm(x, norm2_g[li]), m_l[3], m_l[4]),
                                  peer_wq[li], peer_keys[li], peer_u[li], peer_v[li])
        if update_ctx:
            ctx = ctx + m_c[2] * out_c
            ctx = ctx + m_c[5] * peer_ffn(modulate(rmsnorm(ctx, norm2_g[li]), m_c[3], m_c[4]),
                                          peer_wq[li], peer_keys[li], peer_u[li], peer_v[li])
    return rmsnorm(x, final_g)
```

```python
import math
from contextlib import ExitStack

import numpy as np
import concourse.bass as bass
import concourse.mybir as mybir
from concourse.bass_utils import run_bass_kernel_spmd

F32 = mybir.dt.float32
BF16 = mybir.dt.bfloat16
U32 = mybir.dt.uint32
AF = mybir.ActivationFunctionType
ALU = mybir.AluOpType
AX = mybir.AxisListType

ENGS = ["sync", "scalar", "vector", "gpsimd", "tensor"]

D = 1024
NCH = 8
CTX = 256
LAT = 2048
T = CTX + LAT
NT = T // 128
DEPTH = 4
EPS = 1e-6
BLOCKS = [(0, 256)] + [(256 + 512 * i, 512) for i in range(4)]
NEXP = 16384

PV_N1 = 0
PV_N2 = 8
PV_CW = 16
PV_CB = 48
PV_LB = 56
PV_LL = 88
PV_MB = 104
PV_SG = 152
PV_DL = 153
NPV = 409

DEBUG = {}


class Prog:
    def __init__(self, nc, n_dma_slots=16):
        self.nc = nc
        self.ops = {e: [] for e in ENGS}
        self.cnt = {e: 0 for e in ENGS}
        self.last_w = {}
        self.readers = {}
        self.seen = {e: {} for e in ENGS}
        self.nslots = n_dma_slots
        self.dma_i = {e: 0 for e in ENGS}
        self.dma_last = {}

    def _deps(self, eng, reads, writes):
        deps = {}

        def add(tok):
            if tok is None:
                return
            k, v = tok
            if deps.get(k, 0) < v:
                deps[k] = v
        for r in reads:
            add(self.last_w.get(r))
        for w in writes:
            add(self.last_w.get(w))
            for t in self.readers.get(w, ()):
                add(t)
        waits = []
        for k, v in deps.items():
            if eng == "tensor" and k == ("eng", "tensor"):
                continue
            if self.seen[eng].get(k, 0) >= v:
                continue
            self.seen[eng][k] = v
            waits.append((k, v))
        return waits

    def _commit(self, tok, reads, writes):
        for w in writes:
            self.last_w[w] = tok
            self.readers[w] = []
        for r in reads:
            if r in writes:
                continue
            self.readers.setdefault(r, []).append(tok)

    def op(self, eng, fn, reads=(), writes=()):
        reads = list(reads)
        writes = list(writes)
        waits = self._deps(eng, reads, writes)
        self.cnt[eng] += 1
        tok = (("eng", eng), self.cnt[eng])
        self.ops[eng].append((fn, waits, ("eng", eng), 1))
        self._commit(tok, reads, writes)
        return tok

    def dma(self, eng, fn, reads=(), writes=()):
        reads = list(reads)
        writes = list(writes)
        i = self.dma_i[eng]
        self.dma_i[eng] += 1
        slot = i % self.nslots
        val = 16 * (i // self.nslots + 1)
        key = ("dma", eng, slot)
        waits = self._deps(eng, reads, writes)
        if val > 16:
            prev = val - 16
            if self.seen[eng].get(key, 0) < prev:
                self.seen[eng][key] = prev
                waits.append((key, prev))
        self.ops[eng].append((fn, waits, key, 16))
        tok = (key, val)
        self.dma_last[key] = val
        self._commit(tok, reads, writes)
        return tok

    def barrier(self):
        toks = [(("eng", e), self.cnt[e]) for e in ENGS if self.cnt[e] > 0]
        toks += list(self.dma_last.items())
        for e in ENGS:
            waits = []
            for k, v in toks:
                if self.seen[e].get(k, 0) >= v:
                    continue
                self.seen[e][k] = v
                waits.append((k, v))
            if waits:
                self.ops[e].append((None, waits, None, 0))

    def emit(self):
        nc = self.nc
        self.barrier()
        keys = set()
        for e in ENGS:
            for (_, waits, ik, _) in self.ops[e]:
                if ik is not None:
                    keys.add(ik)
                for k, _ in waits:
                    keys.add(k)
        with ExitStack() as st:
            sems = {}
            for k in sorted(keys, key=str):
                nm = "s_" + "_".join(str(x) for x in k)
                sems[k] = st.enter_context(nc.semaphore(nm))
            block = st.enter_context(nc.Block())

            def mk(e):
                def body(eng):
                    for (fn, waits, ik, inc) in self.ops[e]:
                        for k, v in waits:
                            eng.wait_ge(sems[k], v)
                        if fn is not None:
                            ins = fn(eng)
                            ins.then_inc(sems[ik], inc)
                return body
            for e in ENGS:
                getattr(block, e)(mk(e))


def mm(out, lhsT, rhs, start, stop):
    return lambda e: e.matmul(out, lhsT=lhsT, rhs=rhs, start=start, stop=stop)


def tr(out, in_, ident):
    return lambda e: e.transpose(out, in_, ident)


def act(out, in_, func, **kw):
    return lambda e: e.activation(out=out, in_=in_, func=func, **kw)


def tt(out, in0, in1, op):
    return lambda e: e.tensor_tensor(out=out, in0=in0, in1=in1, op=op)


def ts(out, in0, s1, s2, op0, op1=None):
    if op1 is None:
        return lambda e: e.tensor_scalar(out=out, in0=in0, scalar1=s1, scalar2=None, op0=op0)
    return lambda e: e.tensor_scalar(out=out, in0=in0, scalar1=s1, scalar2=s2, op0=op0, op1=op1)


def stt(out, in0, scalar, in1, op0, op1):
    return lambda e: e.scalar_tensor_tensor(out=out, in0=in0, scalar=scalar, in1=in1, op0=op0, op1=op1)


def cp(out, in_):
    return lambda e: e.tensor_copy(out=out, in_=in_)


def rcp(out, in_):
    return lambda e: e.reciprocal(out=out, in_=in_)


def red(out, in_, op):
    return lambda e: e.tensor_reduce(out=out, in_=in_, axis=AX.X, op=op)


def dm(out, in_):
    return lambda e: e.dma_start(out=out, in_=in_)


_BREG = {}


def gather(out, tab, idx):
    def f(e):
        if "r" not in _BREG:
            _BREG["r"] = e.to_reg(NEXP - 1)
        return e.indirect_dma_start(
            out=out, out_offset=None, in_=tab,
            in_offset=bass.IndirectOffsetOnAxis(ap=idx, axis=0),
            bounds_check=_BREG["r"], oob_is_err=False)
    return f


def build_program(n_layers=DEPTH, stop_phase=None, peer_tiles=None):
    nc = bass.Bass("TRN2", target_bir_lowering=False)
    _BREG.clear()

    def din(name, shape, dt=F32):
        return nc.dram_tensor(name, list(shape), dt, kind="ExternalInput").ap()

    xT0 = din("xT0", [D, T])
    gv = din("gv", [128, 24])
    pvd = din("pv", [DEPTH, 128, NPV])
    modw = din("modw", [DEPTH, 48, 128, 8, 128])
    win = din("win", [DEPTH, 56, 128, 8, 128])
    wbl = din("wbl", [DEPTH, 8, 128, 8, 128])
    wba = din("wba", [DEPTH, 8, 128, 8, 128])
    wo = din("wo", [DEPTH, 8, 128, 8, 128])
    wq = din("wq", [DEPTH, 8, 128, 8, 128])
    lruw = din("lruw", [DEPTH, 8, 128, 4, 128])
    keysbd = din("keysbd", [DEPTH, 128, 8, 256])
    ntab = 8 if DEBUG.get("no_peer_tables") else NEXP
    pu = [din("pu%d" % i, [ntab, D]) for i in range(DEPTH)]
    pvt_ = [din("pvt%d" % i, [ntab, D]) for i in range(DEPTH)]
    cst = din("cst", [128, 128 * 2 + 16])
    ropet = din("ropet", [2, 128, LAT])
    outT = nc.dram_tensor("outT", [D, LAT], F32, kind="ExternalOutput").ap()
    dbg = nc.dram_tensor("dbg", [D, T], F32, kind="ExternalOutput").ap() if DEBUG.get("dump") else None
    zTd = nc.dram_tensor("zTd", [8, 128, T], BF16, kind="Internal").ap()
    oTd = nc.dram_tensor("oTd", [8, 128, T], BF16, kind="Internal").ap()

    ARB = 108544
    with ExitStack() as st:
        def sb(name, shape, dt=F32):
            return st.enter_context(nc.sbuf_tensor(name, list(shape), dt))

        xT = sb("xT", [128, NCH, T])
        arena = sb("arena", [128, ARB // 4])
        cs_t = sb("cs_t", [128, 272])
        identb = sb("identb", [128, 128], BF16)
        permb = sb("permb", [128, 128], BF16)
        onesf = sb("onesf", [128, 128])
        onesb = sb("onesb", [128, 128], BF16)
        pvt = sb("pvt", [128, NPV])
        gvt = sb("gvt", [128, 24])
        sil = sb("sil", [128, 8, 2])
        mT = sb("mT", [128, 48, 2])
        A1 = sb("A1", [128, 8, 2])
        A2 = sb("A2", [128, 8, 2])
        sm = sb("sm", [128, 256])
        lamt = sb("lamt", [128, 4])
        csv = sb("csv", [128, 32])
        sgt = sb("sgt", [128, 1])
        NWB = 6
        wb = [sb("wb%d" % i, [128, 8, 128], BF16) for i in range(NWB)]
        lwb = [sb("lwb%d" % i, [128, 4, 128], BF16) for i in range(2)]
        pb = [st.enter_context(nc.psum_tensor("pb%d" % i, [128, 512], F32)) for i in range(8)]

        ident = cs_t[:, 0:128]
        permf = cs_t[:, 128:256]
        iota16 = cs_t[:, 256:272]

        def av(off, dt, n, shape=None):
            esz = 4 if dt in (F32, U32) else 2
            assert off % 4 == 0 and (n * esz) % 4 == 0 and off + n * esz <= ARB
            v = arena[:, off // 4: (off + n * esz) // 4]
            if dt != F32:
                v = v.bitcast(dt)
            if shape is not None:
                names = " ".join("d%d" % i for i in range(len(shape)))
                kw = {"d%d" % i: s for i, s in enumerate(shape)}
                v = v.rearrange("p (%s) -> p %s" % (names, names), **kw)
            return v

        P = Prog(nc)
        wbi = [0]

        def next_wb():
            i = wbi[0] % NWB
            wbi[0] += 1
            return i

        def load_w(src):
            i = next_wb()
            P.dma("gpsimd", dm(wb[i][:], src), reads=[], writes=[("wb", i)])
            return i

        P.dma("sync", dm(cs_t[:], cst), writes=["cst"])
        P.dma("sync", dm(gvt[:], gv), writes=["gvt"])
        for c in range(NCH):
            P.dma("sync", dm(xT[:, c, :], xT0[c * 128:(c + 1) * 128, :]),
                  writes=[("xT", c, b) for b in range(5)])
        P.op("vector", cp(identb[:], ident), reads=["cst"], writes=["identb"])
        P.op("vector", cp(permb[:], permf), reads=["cst"], writes=["permb"])
        P.op("gpsimd", lambda e: e.memset(onesf[:], 1.0), writes=["onesf"])
        P.op("gpsimd", lambda e: e.memset(onesb[:], 1.0), writes=["onesb"])
        P.op("scalar", act(sil[:, :, 0], gvt[:, 0:8], AF.Silu), reads=["gvt"], writes=["sil"])
        P.op("scalar", act(sil[:, :, 1], gvt[:, 8:16], AF.Silu), reads=["gvt", "sil"], writes=["sil"])
        P.barrier()

        def norm_mod(blk_s0, n, s, Ax, shift_j, dst_fn, key_x, key_dst, sq_off, pbank, bw=512):
            sq = [av(sq_off + i * bw * 4, F32, bw) for i in range(2)]
            r1 = av(sq_off + 2 * bw * 4, F32, bw)
            rstd = av(sq_off + 3 * bw * 4, F32, bw)
            tmp = [av(sq_off + (4 + i) * bw * 4, F32, bw) for i in range(2)]
            for c in range(NCH):
                P.op("scalar", act(sq[c % 2][:, :n], xT[:, c, blk_s0:blk_s0 + n], AF.Square),
                     reads=[key_x(c)], writes=[("sq", c % 2)])
                P.op("tensor", mm(pb[pbank][:, :n], onesf[:], sq[c % 2][:, :n], c == 0, c == NCH - 1),
                     reads=[("sq", c % 2), "onesf"], writes=[("pb", pbank)])
            P.op("scalar", act(r1[:, :n], pb[pbank][:, :n], AF.Sqrt, scale=1.0 / D, bias=epst[:, 0:1]),
                 reads=[("pb", pbank), "epst"], writes=["r1"])
            P.op("vector", rcp(rstd[:, :n], r1[:, :n]), reads=["r1"], writes=["rstd"])
            for c in range(NCH):
                P.op("vector", tt(tmp[c % 2][:, :n], xT[:, c, blk_s0:blk_s0 + n], rstd[:, :n], ALU.mult),
                     reads=[key_x(c), "rstd"], writes=[("ntmp", c % 2)])
                P.op("scalar", act(dst_fn(c), tmp[c % 2][:, :n], AF.Identity,
                                   scale=Ax[:, c, s:s + 1], bias=mT[:, shift_j * 8 + c, s:s + 1]),
                     reads=[("ntmp", c % 2), "mT", "Ax"], writes=[key_dst(c)])

        epst = sb("epst", [128, 2])
        P.op("gpsimd", lambda e: e.memset(epst[:, 0:1], EPS), writes=["epst"])
        P.op("gpsimd", lambda e: e.memset(epst[:, 1:2], 1.0), writes=["epst"])

        hT = av(0, BF16, NCH * T, [NCH, T])
        HB = 36864

        for li in range(n_layers):
            last = (li == DEPTH - 1)
            lambda_init = 0.8 - 0.6 * math.exp(-0.3 * li)
            blocks = [(i, s0, n) for i, (s0, n) in enumerate(BLOCKS)]
            oblocks = [b for b in blocks if not (last and b[0] == 0)]

            P.dma("sync", dm(pvt[:], pvd[li]), writes=["pvt"])
            wst = [av(HB + i * 4096, F32, 1024, [8, 128]) for i in range(2)]
            psm = pb[0][:, 0:96]
            for jc in range(48):
                P.dma("sync", dm(wst[jc % 2], modw[li, jc]), writes=[("wst", jc % 2)])
                for kc in range(8):
                    P.op("tensor", mm(psm[:, jc * 2:(jc + 1) * 2], wst[jc % 2][:, kc, :], sil[:, kc, :], kc == 0, kc == 7),
                         reads=[("wst", jc % 2), "sil"], writes=[("pb", 0)])
            P.op("vector", tt(mT[:], psm.rearrange("p (a b) -> p a b", b=2),
                              pvt[:, PV_MB:PV_MB + 48].unsqueeze(2).to_broadcast([128, 48, 2]), ALU.add),
                 reads=[("pb", 0), "pvt"], writes=["mT"])
            for (Ax, j, pg, nm) in ((A1, 1, PV_N1, "A1"), (A2, 4, PV_N2, "A2")):
                P.op("vector", ts(Ax[:], mT[:, j * 8:(j + 1) * 8, :], 1.0, None, ALU.add), reads=["mT"], writes=[nm])
                P.op("vector", tt(Ax[:], Ax[:], pvt[:, pg:pg + 8].unsqueeze(2).to_broadcast([128, 8, 2]), ALU.mult),
                     reads=[nm, "pvt"], writes=[nm])
            dl = pvt[:, PV_DL:PV_DL + 256]
            P.op("vector", tt(sm[:, 0:64], dl[:, 0:64], dl[:, 64:128], ALU.mult), reads=["pvt"], writes=["sm"])
            P.op("vector", tt(sm[:, 64:128], dl[:, 128:192], dl[:, 192:256], ALU.mult), reads=["pvt", "sm"], writes=["sm"])
            P.op("vector", red(sm[:, 128:130], sm[:, 0:128].rearrange("p (a b) -> p a b", b=64), ALU.add),
                 reads=["sm"], writes=["sm"])
            P.op("scalar", act(sm[:, 130:132], sm[:, 128:130], AF.Exp), reads=["sm"], writes=["sm"])
            P.op("vector", ts(lamt[:, 0:1], sm[:, 130:131], sm[:, 131:132], float(lambda_init), ALU.subtract, ALU.add),
                 reads=["sm"], writes=["lamt"])
            P.op("vector", ts(lamt[:, 1:2], lamt[:, 0:1], -1.0, None, ALU.mult), reads=["lamt"], writes=["lamt"])
            P.op("vector", ts(sgt[:], pvt[:, PV_SG:PV_SG + 1], float(1.0 - lambda_init), None, ALU.mult),
                 reads=["pvt"], writes=["sgt"])
            P.op("scalar", act(sm[:, 136:152], pvt[:, PV_LL:PV_LL + 16], AF.Sigmoid), reads=["pvt", "sm"], writes=["sm"])
            P.op("scalar", act(sm[:, 136:152], sm[:, 136:152], AF.Ln), reads=["sm"], writes=["sm"])
            P.op("vector", ts(csv[:, 0:16], sm[:, 136:152], 8.0, None, ALU.mult), reads=["sm"], writes=["csv"])
            P.op("vector", ts(csv[:, 16:32], sm[:, 136:152], 16.0, None, ALU.mult), reads=["sm", "csv"], writes=["csv"])
            P.barrier()

            for (bi, s0, n) in blocks:
                s = 1 if bi == 0 else 0
                norm_mod(s0, n, s, A1, 0,
                         lambda c, s0=s0, n=n: hT[:, c, s0:s0 + n],
                         lambda c, bi=bi: ("xT", c, bi), lambda c, bi=bi: ("hT", c, bi),
                         HB, bi % 2)
            P.barrier()

            def inproj_fm(widx, bi, s0, n, pbank):
                for kc in range(8):
                    P.op("tensor", mm(pb[pbank][:, :n], wb[widx][:, kc, :], hT[:, kc, s0:s0 + n], kc == 0, kc == 7),
                         reads=[("wb", widx), ("hT", kc, bi)], writes=[("pb", pbank)])

            if stop_phase != "A" and not DEBUG.get("skip_b"):
                uc = av(HB, F32, T)
                Ab = av(HB + 9216, F32, T)
                Bb = av(HB + 18432, F32, T)
                C0 = av(HB + 27648, F32, T)
                C1 = av(HB + 36864, F32, T)
                ucb = av(HB + 46080, BF16, T)
                zc = av(HB + 50688, BF16, T)
                for c in range(NCH):
                    wu = load_w(win[li, 0 + c])
                    wg = load_w(win[li, 8 + c])
                    lw = c % 2
                    P.dma("gpsimd", dm(lwb[lw][:], lruw[li, c]), writes=[("lwb", lw)])
                    for (bi, s0, n) in blocks:
                        inproj_fm(wu, bi, s0, n, bi % 2)
                        P.op("scalar", act(C1[:, s0:s0 + n], pb[bi % 2][:, :n], AF.Identity),
                             reads=[("pb", bi % 2)], writes=[("C1", bi)])
                    allb = range(5)
                    cw = lambda k: pvt[:, PV_CW + k * 8 + c: PV_CW + k * 8 + c + 1]
                    for (a, b, bl) in ((0, CTX, [0]), (CTX, T, [1, 2, 3, 4])):
                        rk = [("C1", i) for i in bl]
                        wk_ = [("uc", i) for i in bl]
                        P.op("vector", ts(uc[:, a:b], C1[:, a:b], cw(2), pvt[:, PV_CB + c:PV_CB + c + 1], ALU.mult, ALU.add),
                             reads=rk + ["pvt"], writes=wk_)
                        P.op("vector", stt(uc[:, a + 2:b], C1[:, a:b - 2], cw(0), uc[:, a + 2:b], ALU.mult, ALU.add),
                             reads=rk + ["pvt"] + wk_, writes=wk_)
                        P.op("vector", stt(uc[:, a + 1:b], C1[:, a:b - 1], cw(1), uc[:, a + 1:b], ALU.mult, ALU.add),
                             reads=rk + ["pvt"] + wk_, writes=wk_)
                        P.op("vector", stt(uc[:, a:b - 1], C1[:, a + 1:b], cw(3), uc[:, a:b - 1], ALU.mult, ALU.add),
                             reads=rk + ["pvt"] + wk_, writes=wk_)
                    ucK = [("uc", i) for i in allb]
                    P.op("gpsimd", cp(ucb[:], uc[:]), reads=ucK, writes=["ucb"])
                    for d_ in range(2):
                        Cd = C0 if d_ == 0 else C1
                        CdK = [("C0" if d_ == 0 else "C1", i) for i in allb]
                        for (bi, s0, n) in blocks:
                            for g_, (dst, nm) in enumerate(((Ab, "Ab"), (Bb, "Bb"))):
                                pbk = 2 + g_ * 2 + (bi % 2)
                                P.op("tensor", mm(pb[pbk][:, :n], lwb[lw][:, d_ * 2 + g_, :], ucb[:, s0:s0 + n], True, True),
                                     reads=[("lwb", lw), "ucb"], writes=[("pb", pbk)])
                                bcol = PV_LB + (d_ * 2 + g_) * 8 + c
                                P.op("scalar", act(dst[:, s0:s0 + n], pb[pbk][:, :n], AF.Sigmoid, bias=pvt[:, bcol:bcol + 1]),
                                     reads=[("pb", pbk), "pvt"], writes=[(nm, bi)])
                        AbK = [("Ab", i) for i in allb]
                        BbK = [("Bb", i) for i in allb]
                        csc = csv[:, d_ * 8 + c: d_ * 8 + c + 1]
                        cs2c = csv[:, 16 + d_ * 8 + c: 16 + d_ * 8 + c + 1]
                        P.op("scalar", act(Cd[:], Ab[:], AF.Exp, scale=cs2c), reads=AbK + ["csv"], writes=CdK)
                        P.op("scalar", act(Ab[:], Ab[:], AF.Exp, scale=csc), reads=AbK + ["csv"], writes=AbK)
                        P.op("scalar", act(Cd[:], Cd[:], AF.Sqrt, scale=-1.0, bias=epst[:, 1:2]), reads=CdK + ["epst"], writes=CdK)
                        P.op("gpsimd", tt(Bb[:], Bb[:], uc[:], ALU.mult), reads=BbK + ucK, writes=BbK)
                        P.op("vector", tt(Bb[:], Bb[:], Cd[:], ALU.mult), reads=BbK + CdK, writes=BbK)
                        if d_ == 0:
                            P.op("vector", lambda e, Cd=Cd: e.tensor_tensor_scan(
                                out=Cd[:], data0=Ab[:], data1=Bb[:], initial=0.0, op0=ALU.mult, op1=ALU.add),
                                reads=AbK + BbK, writes=CdK)
                        else:
                            P.op("vector", lambda e, Cd=Cd: e.tensor_tensor_scan(
                                out=Cd[:, 0:CTX][:, ::-1], data0=Ab[:, 0:CTX][:, ::-1], data1=Bb[:, 0:CTX][:, ::-1],
                                initial=0.0, op0=ALU.mult, op1=ALU.add),
                                reads=AbK + BbK, writes=CdK)
                            P.op("vector", lambda e, Cd=Cd: e.tensor_tensor_scan(
                                out=Cd[:, CTX:T][:, ::-1], data0=Ab[:, CTX:T][:, ::-1], data1=Bb[:, CTX:T][:, ::-1],
                                initial=Cd[:, 0:1], op0=ALU.mult, op1=ALU.add),
                                reads=AbK + BbK + CdK, writes=CdK)
                    C0K = [("C0", i) for i in allb]
                    C1K = [("C1", i) for i in allb]
                    P.op("gpsimd", tt(C0[:], C0[:], C1[:], ALU.add), reads=C0K + C1K, writes=C0K)
                    for (bi, s0, n) in blocks:
                        inproj_fm(wg, bi, s0, n, bi % 2)
                        P.op("scalar", act(Ab[:, s0:s0 + n], pb[bi % 2][:, :n], AF.Gelu_apprx_tanh),
                             reads=[("pb", bi % 2)], writes=[("Ab", bi)])
                    P.op("vector", tt(zc[:], C0[:], Ab[:], ALU.mult), reads=C0K + [("Ab", i) for i in allb], writes=["zc"])
                    P.dma("sync", dm(zTd[c], zc[:]), reads=["zc"], writes=[("zTd", c)])
                P.barrier()

            if stop_phase not in ("A", "B"):
                cosT = av(HB, F32, LAT)
                sinT = av(HB + 8192, F32, LAT)
                qTt = av(HB + 16384, BF16, T)
                kTt = av(HB + 20992, BF16, T)
                vh = av(HB + 25600, BF16, NT * 128, [NT, 128])
                qraw = av(HB + 30208, BF16, 512)
                pT = [av(HB + 31232 + i * 1024, BF16, 512) for i in range(3)]
                wk = [av(HB + 34304 + i * 2048, F32, 512) for i in range(6)]
                oTc = av(HB + 46592, BF16, T)
                P.dma("sync", dm(cosT, ropet[0]), writes=["cosT"])
                P.dma("sync", dm(sinT, ropet[1]), writes=["sinT"])
                pti = [0]
                for h in range(8):
                    wqi = load_w(win[li, 16 + h])
                    wki = load_w(win[li, 24 + h])
                    wvi = load_w(win[li, 32 + h])
                    for (widx, dstT, nm) in ((wqi, qTt, "qT"), (wki, kTt, "kT")):
                        for (bi, s0, n) in (blocks if not DEBUG.get("skip_qk") else []):
                            if nm == "qT" and last and bi == 0:
                                continue
                            inproj_fm(widx, bi, s0, n, 7)
                            if bi == 0 or DEBUG.get("norope"):
                                P.op("scalar", act(dstT[:, s0:s0 + n], pb[7][:, :n], AF.Identity),
                                     reads=[("pb", 7)], writes=[(nm, bi)])
                            else:
                                l0 = s0 - CTX
                                P.op("scalar", act(qraw[:, :n], pb[7][:, :n], AF.Identity), reads=[("pb", 7)], writes=["qraw"])
                                P.op("tensor", mm(pb[6][:, :n], permb[:], qraw[:, :n], True, True),
                                     reads=["qraw", "permb"], writes=[("pb", 6)])
                                P.op("scalar", act(wk[2][:, :n], pb[7][:, :n], AF.Identity), reads=[("pb", 7)], writes=[("wk", 2)])
                                P.op("scalar", act(wk[3][:, :n], pb[6][:, :n], AF.Identity), reads=[("pb", 6)], writes=[("wk", 3)])
                                P.op("vector", tt(wk[0][:, :n], wk[2][:, :n], cosT[:, l0:l0 + n], ALU.mult),
                                     reads=[("wk", 2), "cosT"], writes=[("wk", 0)])
                                P.op("gpsimd", tt(wk[1][:, :n], wk[3][:, :n], sinT[:, l0:l0 + n], ALU.mult),
                                     reads=[("wk", 3), "sinT"], writes=[("wk", 1)])
                                P.op("vector", tt(dstT[:, s0:s0 + n], wk[0][:, :n], wk[1][:, :n], ALU.add),
                                     reads=[("wk", 0), ("wk", 1)], writes=[(nm, bi)])
                    for t_ in (range(NT) if not DEBUG.get("skip_v") else []):
                        bi = 0 if t_ < 2 else 1 + (t_ - 2) // 4
                        pbk = 6 + (t_ % 2)
                        for kc in range(8):
                            P.op("tensor", mm(pb[pbk][:, 0:128], hT[:, kc, t_ * 128:(t_ + 1) * 128], wb[wvi][:, kc, :], kc == 0, kc == 7),
                                 reads=[("wb", wvi), ("hT", kc, bi)], writes=[("pb", pbk)])
                        P.op("scalar", act(vh[:, t_, :], pb[pbk][:, 0:128], AF.Identity), reads=[("pb", pbk)], writes=[("vh", t_)])
                    for (bi, s0, n) in (oblocks if DEBUG.get("c_sub", 3) >= 2 else []):
                        ktiles = [0, 1] if bi == 0 else list(range(NT))
                        for m_ in range(2):
                            pso, psz = 2 + 2 * m_, 3 + 2 * m_
                            for ki, kt in enumerate(ktiles):
                                kbi = 0 if kt < 2 else 1 + (kt - 2) // 4
                                j = pti[0] % 2
                                j2 = pti[0] % 3
                                pti[0] += 1
                                P.op("tensor", mm(pb[j][:, :n], kTt[m_ * 64:(m_ + 1) * 64, kt * 128:(kt + 1) * 128],
                                                  qTt[m_ * 64:(m_ + 1) * 64, s0:s0 + n], True, True),
                                     reads=[("kT", kbi), ("qT", bi)], writes=[("pb", j)])
                                P.op("scalar", act(pT[j2][:, :n], pb[j][:, :n], AF.Exp, scale=0.125),
                                     reads=[("pb", j)], writes=[("pT", j2)])
                                P.op("tensor", mm(pb[pso][:, :n], vh[:, kt, :], pT[j2][:, :n], ki == 0, ki == len(ktiles) - 1),
                                     reads=[("vh", kt), ("pT", j2)], writes=[("pb", pso)])
                                P.op("tensor", mm(pb[psz][:, :n], onesb[:], pT[j2][:, :n], ki == 0, ki == len(ktiles) - 1),
                                     reads=["onesb", ("pT", j2)], writes=[("pb", psz)])
                        if DEBUG.get("c_sub", 3) < 3:
                            continue
                        W = lambda i: wk[i][:, :n]
                        P.op("vector", rcp(W(0), pb[3][:, :n]), reads=[("pb", 3)], writes=[("wk", 0)])
                        P.op("vector", tt(W(1), pb[2][:, :n], W(0), ALU.mult), reads=[("pb", 2), ("wk", 0)], writes=[("wk", 1)])
                        P.op("vector", rcp(W(2), pb[5][:, :n]), reads=[("pb", 5)], writes=[("wk", 2)])
                        P.op("vector", tt(W(3), pb[4][:, :n], W(2), ALU.mult), reads=[("pb", 4), ("wk", 2)], writes=[("wk", 3)])
                        P.op("vector", stt(W(4), W(3), lamt[:, 1:2], W(1), ALU.mult, ALU.add),
                             reads=[("wk", 3), ("wk", 1), "lamt"], writes=[("wk", 4)])
                        P.op("scalar", act(W(5), W(4), AF.Square), reads=[("wk", 4)], writes=[("wk", 5)])
                        P.op("tensor", mm(pb[6][:, :n], onesf[:], W(5), True, True), reads=["onesf", ("wk", 5)], writes=[("pb", 6)])
                        P.op("scalar", act(W(0), pb[6][:, :n], AF.Sqrt, scale=1.0 / 128, bias=epst[:, 0:1]),
                             reads=[("pb", 6), "epst"], writes=[("wk", 0)])
                        P.op("vector", rcp(W(2), W(0)), reads=[("wk", 0)], writes=[("wk", 2)])
                        P.op("vector", tt(W(1), W(4), W(2), ALU.mult), reads=[("wk", 4), ("wk", 2)], writes=[("wk", 1)])
                        P.op("scalar", act(oTc[:, s0:s0 + n], W(1), AF.Identity, scale=sgt[:, 0:1]),
                             reads=[("wk", 1), "sgt"], writes=[("oTc", bi)])
                    P.dma("sync", dm(oTd[h], oTc[:]), reads=[("oTc", i) for i in range(5)], writes=[("oTd", h)])
                P.barrier()

            if stop_phase not in ("A", "B", "C"):
                zb = av(HB, BF16, 8 * 512, [8, 512])
                ob = av(HB + 8192, BF16, 8 * 512, [8, 512])
                merged = av(HB + 16384, BF16, NCH * T, [NCH, T])
                wk = [av(HB + 53248 + i * 2048, F32, 512) for i in range(4)]
                for m_ in range(8):
                    wA = load_w(wbl[li, m_])
                    wB = load_w(wba[li, m_])
                    wGa = load_w(win[li, 40 + m_])
                    wGb = load_w(win[li, 48 + m_])
                    for (bi, s0, n) in oblocks:
                        P.dma("sync", dm(zb[:, :, :n], zTd[:, :, s0:s0 + n].rearrange("c p t -> p c t")),
                              reads=[("zTd", c) for c in range(8)], writes=["zb"])
                        P.dma("sync", dm(ob[:, :, :n], oTd[:, :, s0:s0 + n].rearrange("c p t -> p c t")),
                              reads=[("oTd", c) for c in range(8)], writes=["ob"])
                        for kc in range(8):
                            P.op("tensor", mm(pb[0][:, :n], wb[wA][:, kc, :], zb[:, kc, :n], kc == 0, kc == 7),
                                 reads=[("wb", wA), "zb"], writes=[("pb", 0)])
                        for kc in range(8):
                            P.op("tensor", mm(pb[1][:, :n], wb[wB][:, kc, :], ob[:, kc, :n], kc == 0, kc == 7),
                                 reads=[("wb", wB), "ob"], writes=[("pb", 1)])
                        inproj_fm(wGa, bi, s0, n, 2)
                        inproj_fm(wGb, bi, s0, n, 3)
                        W = lambda i: wk[i][:, :n]
                        P.op("scalar", act(W(0), pb[2][:, :n], AF.Sigmoid), reads=[("pb", 2)], writes=[("wk", 0)])
                        P.op("scalar", act(W(1), pb[3][:, :n], AF.Sigmoid), reads=[("pb", 3)], writes=[("wk", 1)])
                        P.op("vector", tt(W(2), W(0), pb[0][:, :n], ALU.mult), reads=[("pb", 0), ("wk", 0)], writes=[("wk", 2)])
                        P.op("vector", tt(W(3), W(1), pb[1][:, :n], ALU.mult), reads=[("pb", 1), ("wk", 1)], writes=[("wk", 3)])
                        P.op("gpsimd", tt(merged[:, m_, s0:s0 + n], W(2), W(3), ALU.add),
                             reads=[("wk", 2), ("wk", 3)], writes=[("mg", m_, bi)])
                for m_ in range(8):
                    wO = load_w(wo[li, m_])
                    for (bi, s0, n) in oblocks:
                        s = 1 if bi == 0 else 0
                        pbk = 4 + bi % 2
                        for kc in range(8):
                            P.op("tensor", mm(pb[pbk][:, :n], wb[wO][:, kc, :], merged[:, kc, s0:s0 + n], kc == 0, kc == 7),
                                 reads=[("wb", wO), ("mg", kc, bi)], writes=[("pb", pbk)])
                        P.op("vector", stt(xT[:, m_, s0:s0 + n], pb[pbk][:, :n], mT[:, 16 + m_, s:s + 1], xT[:, m_, s0:s0 + n],
                                           ALU.mult, ALU.add),
                             reads=[("pb", pbk), "mT", ("xT", m_, bi)], writes=[("xT", m_, bi)])
                P.barrier()

            if stop_phase not in ("A", "B", "C", "D"):
                wqb = av(0, BF16, 8 * 8 * 128, [8, 8, 128])
                keysf = av(16384, F32, 8 * 256, [8, 256])
                hn2T = av(24576, BF16, 8 * 128, [8, 128])
                qpT = av(26624, F32, 8 * 128, [8, 128])
                ytok = av(30720, F32, 1024)
                sc = av(34816, F32, 2048)
                sc2 = av(43008, F32, 2048)
                tmpb = av(51200, F32, 2048)
                SO = 59392
                v16 = av(SO, F32, 256, [16, 16])
                i16 = av(SO + 1024, U32, 256, [16, 16])
                i16f = av(SO + 2048, F32, 256, [16, 16])
                best = av(SO + 3072, F32, 128, [8, 16])
                posu = av(SO + 3584, U32, 128)
                k1u = av(SO + 4096, U32, 128)
                k2u = av(SO + 4608, U32, 128)
                k1f = av(SO + 5120, F32, 128, [8, 16])
                k2f = av(SO + 5632, F32, 128, [8, 16])
                i1s = av(SO + 6144, F32, 128)
                i2s = av(SO + 6656, F32, 128)
                idxf = av(SO + 7168, F32, 128)
                idxu = av(SO + 7680, U32, 128)
                gt = av(SO + 8192, F32, 128, [8, 16])
                gs = av(SO + 8704, F32, 8)
                actv = av(SO + 8768, F32, 128)
                coef = av(SO + 9280, F32, 128)
                NDG = 4
                diag = [av(SO + 9792 + i * 256, BF16, 128) for i in range(NDG)]
                junk = av(SO + 10816, BF16, 1024)
                GB = SO + 12864
                NG = 8
                Ug = [av(GB + i * 2048, BF16, 1024) for i in range(NG)]
                Vg = [av(GB + NG * 2048 + i * 2048, BF16, 1024) for i in range(NG)]
                sqo = 24576
                for m_ in range(8):
                    P.dma("gpsimd", dm(wqb[:, m_, :, :], wq[li, m_]), writes=[("wqb", m_)])
                P.dma("sync", dm(keysf, keysbd[li]), writes=["keysf"])
                tiles = list(range(NT))
                if last:
                    tiles = tiles[2:]
                if peer_tiles is not None:
                    tiles = [t_ for t_ in tiles if t_ in peer_tiles]
                for t_ in tiles:
                    s = 1 if t_ < 2 else 0
                    t0 = t_ * 128
                    norm_mod(t0, 128, s, A2, 3,
                             lambda c: hn2T[:, c, :],
                             lambda c, t_=t_: ("xT", c, t_), lambda c: ("hn2T", c),
                             105024, 0, bw=128)
                    for hh in range(8):
                        pbk = 1 + hh // 4
                        for kc in range(8):
                            P.op("tensor", mm(pb[pbk][:, (hh % 4) * 128:(hh % 4 + 1) * 128], wqb[:, hh, kc, :], hn2T[:, kc, :], kc == 0, kc == 7),
                                 reads=[("wqb", hh), ("hn2T", kc)], writes=[("pb", pbk)])
                    for g_ in range(2):
                        P.op("scalar", act(qpT[:, g_ * 4:(g_ + 1) * 4, :], pb[1 + g_][:].rearrange("p (a b) -> p a b", b=128), AF.Identity),
                             reads=[("pb", 1 + g_)], writes=[("qpT", g_)])
                    for hh in range(8):
                        pbk = 3 + hh // 2
                        P.op("tensor", mm(pb[pbk][:, (hh % 2) * 256:(hh % 2 + 1) * 256], qpT[:, hh, :], keysf[:, hh, :], True, True),
                             reads=[("qpT", hh // 4), "keysf"], writes=[("pb", pbk)])
                    for j in range(4):
                        P.op("scalar", act(sc[:, j * 512:(j + 1) * 512], pb[3 + j][:], AF.Identity),
                             reads=[("pb", 3 + j)], writes=["sc"])
                    psT = pb[7][:].bitcast(BF16)
                    for c in range(8):
                        P.op("tensor", tr(psT[:, c * 128:(c + 1) * 128], hn2T[:, c, :], identb[:]),
                             reads=[("hn2T", c), "identb"], writes=[("pb", 7)])
                    sc3 = sc.rearrange("p (g n) -> p g n", n=128)
                    sc23 = sc2.rearrange("p (g n) -> p g n", n=128)
                    for g_ in range(16):
                        P.op("vector", lambda e, g_=g_: e.max(out=v16[:, g_, 0:8], in_=sc3[:, g_, :]), reads=["sc"], writes=["v16"])
                        P.op("vector", lambda e, g_=g_: e.match_replace(out=sc23[:, g_, :], in_to_replace=v16[:, g_, 0:8],
                                                                      in_values=sc3[:, g_, :], imm_value=-1e30),
                             reads=["sc", "v16"], writes=["sc2"])
                        P.op("vector", lambda e, g_=g_: e.max(out=v16[:, g_, 8:16], in_=sc23[:, g_, :]), reads=["sc2", "v16"], writes=["v16"])
                        P.op("vector", lambda e, g_=g_: e.max_index(out=i16[:, g_, 0:8], in_max=v16[:, g_, 0:8], in_values=sc3[:, g_, :]),
                             reads=["sc", "v16"], writes=["i16"])
                        P.op("vector", lambda e, g_=g_: e.max_index(out=i16[:, g_, 8:16], in_max=v16[:, g_, 8:16], in_values=sc3[:, g_, :]),
                             reads=["sc", "v16", "i16"], writes=["i16"])
                    v4 = v16.rearrange("p (h two) k -> p h two k", two=2)
                    cand = tmpb.rearrange("p (h a b) -> p h a b", a=16, b=16)
                    P.op("vector", tt(cand, v4[:, :, 0, :].unsqueeze(3).to_broadcast([128, 8, 16, 16]),
                                      v4[:, :, 1, :].unsqueeze(2).to_broadcast([128, 8, 16, 16]), ALU.add),
                         reads=["v16"], writes=["tmpb"])
                    cand2 = tmpb.rearrange("p (h n) -> p h n", n=256)
                    sc2c = sc2.rearrange("p (h n) -> p h n", n=256)
                    pos3 = posu.rearrange("p (h k) -> p h k", k=16)
                    for hh in range(8):
                        P.op("vector", lambda e, hh=hh: e.max(out=best[:, hh, 0:8], in_=cand2[:, hh, :]), reads=["tmpb"], writes=["best"])
                        P.op("vector", lambda e, hh=hh: e.match_replace(out=sc2c[:, hh, :], in_to_replace=best[:, hh, 0:8],
                                                                      in_values=cand2[:, hh, :], imm_value=-1e30),
                             reads=["tmpb", "best"], writes=["sc2"])
                        P.op("vector", lambda e, hh=hh: e.max(out=best[:, hh, 8:16], in_=sc2c[:, hh, :]), reads=["sc2", "best"], writes=["best"])
                        P.op("vector", lambda e, hh=hh: e.max_index(out=pos3[:, hh, 0:8], in_max=best[:, hh, 0:8], in_values=cand2[:, hh, :]),
                             reads=["tmpb", "best"], writes=["posu"])
                        P.op("vector", lambda e, hh=hh: e.max_index(out=pos3[:, hh, 8:16], in_max=best[:, hh, 8:16], in_values=cand2[:, hh, :]),
                             reads=["tmpb", "best", "posu"], writes=["posu"])
                    P.op("vector", lambda e: e.tensor_single_scalar(out=k1u, in_=posu, scalar=4, op=ALU.logical_shift_right),
                         reads=["posu"], writes=["k1u"])
                    P.op("vector", lambda e: e.tensor_single_scalar(out=k2u, in_=posu, scalar=15, op=ALU.bitwise_and),
                         reads=["posu"], writes=["k2u"])
                    P.op("vector", cp(k1f.rearrange("p h k -> p (h k)"), k1u), reads=["k1u"], writes=["k1f"])
                    P.op("vector", cp(k2f.rearrange("p h k -> p (h k)"), k2u), reads=["k2u"], writes=["k2f"])
                    P.op("vector", cp(i16f, i16), reads=["i16"], writes=["i16f"])
                    i4 = i16f.rearrange("p (h two) k -> p h two k", two=2)
                    iob = iota16.unsqueeze(1).unsqueeze(1).to_broadcast([128, 8, 16, 16])
                    oh = sc.rearrange("p (h a b) -> p h a b", a=16, b=16)
                    for (kf, which, dst, nm) in ((k1f, 0, i1s, "i1s"), (k2f, 1, i2s, "i2s")):
                        P.op("vector", tt(oh, kf.unsqueeze(3).to_broadcast([128, 8, 16, 16]), iob, ALU.is_equal),
                             reads=["k1f", "k2f", "cst", "sc"], writes=["sc"])
                        P.op("vector", tt(oh, oh, i4[:, :, which, :].unsqueeze(2).to_broadcast([128, 8, 16, 16]), ALU.mult),
                             reads=["sc", "i16f"], writes=["sc"])
                        P.op("vector", red(dst, sc.rearrange("p (a b) -> p a b", b=16), ALU.add), reads=["sc"], writes=[nm])
                    P.op("vector", stt(idxf, i1s, 128.0, i2s, ALU.mult, ALU.add), reads=["i1s", "i2s"], writes=["idxf"])
                    P.op("vector", cp(idxu, idxf), reads=["idxf"], writes=["idxu"])
                    P.op("vector", tt(gt, best, best[:, :, 0:1].to_broadcast([128, 8, 16]), ALU.subtract), reads=["best"], writes=["gt"])
                    P.op("scalar", act(gt, gt, AF.Exp), reads=["gt"], writes=["gt"])
                    P.op("vector", red(gs, gt, ALU.add), reads=["gt"], writes=["gs"])
                    P.op("vector", rcp(gs, gs), reads=["gs"], writes=["gs"])
                    P.op("vector", tt(gt, gt, gs.unsqueeze(2).to_broadcast([128, 8, 16]), ALU.mult), reads=["gt", "gs"], writes=["gt"])
                    for j in range(128):
                        b_ = j % NG
                        P.dma("gpsimd", gather(Ug[b_], pu[li], idxu[:, j:j + 1]), reads=["idxu"], writes=[("Ug", b_)])
                        P.op("vector", lambda e, b_=b_, j=j: e.scalar_tensor_tensor(
                            out=junk, in0=psT, scalar=1.0, in1=Ug[b_], op0=ALU.mult, op1=ALU.mult,
                            accum_out=actv[:, j:j + 1]),
                            reads=[("pb", 7), ("Ug", b_)], writes=["junk", ("actv", j)])
                    actK = [("actv", j) for j in range(128)]
                    P.op("scalar", act(coef, actv, AF.Gelu_apprx_tanh), reads=actK, writes=["coef"])
                    P.op("vector", tt(coef, coef, gt.rearrange("p h k -> p (h k)"), ALU.mult), reads=["coef", "gt"], writes=["coef"])
                    for j in range(128):
                        b_ = j % NG
                        d_ = j % NDG
                        P.dma("gpsimd", gather(Vg[b_], pvt_[li], idxu[:, j:j + 1]), reads=["idxu"], writes=[("Vg", b_)])
                        P.op("scalar", act(diag[d_], ident, AF.Identity, scale=coef[:, j:j + 1]),
                             reads=["coef", "cst"], writes=[("diag", d_)])
                        for hf in range(2):
                            P.op("tensor", mm(pb[1 + hf][:], diag[d_], Vg[b_][:, hf * 512:(hf + 1) * 512], j == 0, j == 127),
                                 reads=[("diag", d_), ("Vg", b_)], writes=[("pb", 1 + hf)])
                    for hf in range(2):
                        P.op("scalar", act(ytok[:, hf * 512:(hf + 1) * 512], pb[1 + hf][:], AF.Identity),
                             reads=[("pb", 1 + hf)], writes=[("ytok", hf)])
                    for c in range(8):
                        pbk = 3 + c % 2
                        P.op("tensor", tr(pb[pbk][:, 0:128], ytok[:, c * 128:(c + 1) * 128], ident),
                             reads=[("ytok", c // 4), "cst"], writes=[("pb", pbk)])
                        P.op("vector", stt(xT[:, c, t0:t0 + 128], pb[pbk][:, 0:128], mT[:, 40 + c, s:s + 1], xT[:, c, t0:t0 + 128],
                                           ALU.mult, ALU.add),
                             reads=[("pb", pbk), "mT", ("xT", c, t_)], writes=[("xT", c, t_)])
                P.barrier()

        if dbg is not None and DEBUG.get("dump") == "hT":
            dtmp = [av(ARB - 4096 + i * 2048, F32, 512) for i in range(2)]
            k_ = 0
            for c in range(NCH):
                for (s0, n) in BLOCKS:
                    P.op("scalar", act(dtmp[k_ % 2][:, :n], hT[:, c, s0:s0 + n], AF.Identity), reads=[], writes=[("dtmp", k_ % 2)])
                    P.dma("sync", dm(dbg[c * 128:(c + 1) * 128, s0:s0 + n], dtmp[k_ % 2][:, :n]), reads=[("dtmp", k_ % 2)], writes=[("dbg", c, s0)])
                    k_ += 1
        elif dbg is not None and DEBUG.get("dump") in ("zT", "oT"):
            src = zTd if DEBUG.get("dump") == "zT" else oTd
            dtmp = [av(ARB - 4096 + i * 2048, F32, 512) for i in range(2)]
            dtb = [av(ARB - 8192 + i * 1024, BF16, 512) for i in range(2)]
            k_ = 0
            for c in range(NCH):
                for (s0, n) in BLOCKS:
                    P.dma("sync", dm(dtb[k_ % 2][:, :n], src[c, :, s0:s0 + n]), reads=[], writes=[("dtb", k_ % 2)])
                    P.op("scalar", act(dtmp[k_ % 2][:, :n], dtb[k_ % 2][:, :n], AF.Identity), reads=[("dtb", k_ % 2)], writes=[("dtmp", k_ % 2)])
                    P.dma("sync", dm(dbg[c * 128:(c + 1) * 128, s0:s0 + n], dtmp[k_ % 2][:, :n]), reads=[("dtmp", k_ % 2)], writes=[("dbg", c, s0)])
                    k_ += 1
        elif dbg is not None:
            for c in range(NCH):
                P.dma("sync", dm(dbg[c * 128:(c + 1) * 128, :], xT[:, c, :]), reads=[], writes=[("dbg", c)])
        fg = gvt[:, 16:24]
        sq = [av(HB + i * 2048, F32, 512) for i in range(2)]
        r1 = av(HB + 4096, F32, 512)
        rstd = av(HB + 6144, F32, 512)
        ob_ = [av(HB + 8192 + i * 2048, F32, 512) for i in range(4)]
        for (bi, (s0, n)) in enumerate(BLOCKS):
            if bi == 0:
                continue
            for c in range(NCH):
                P.op("scalar", act(sq[c % 2][:, :n], xT[:, c, s0:s0 + n], AF.Square), reads=[], writes=[("fsq", c % 2)])
                P.op("tensor", mm(pb[bi % 2][:, :n], onesf[:], sq[c % 2][:, :n], c == 0, c == NCH - 1),
                     reads=[("fsq", c % 2)], writes=[("pb", bi % 2)])
            P.op("scalar", act(r1[:, :n], pb[bi % 2][:, :n], AF.Sqrt, scale=1.0 / D, bias=epst[:, 0:1]),
                 reads=[("pb", bi % 2)], writes=["fr1"])
            P.op("vector", rcp(rstd[:, :n], r1[:, :n]), reads=["fr1"], writes=["frstd"])
            for c in range(NCH):
                o_ = ob_[c % 4]
                P.op("vector", stt(o_[:, :n], xT[:, c, s0:s0 + n], fg[:, c:c + 1], rstd[:, :n], ALU.mult, ALU.mult),
                     reads=["frstd"], writes=[("fo", c % 4)])
                P.dma("sync", dm(outT[c * 128:(c + 1) * 128, s0 - CTX:s0 - CTX + n], o_[:, :n]),
                      reads=[("fo", c % 4)], writes=[("outT", c, bi)])
        P.emit()
    return nc


def _lay(W):
    K, N = W.shape
    return np.ascontiguousarray(W.reshape(8, 128, N // 128, 128).transpose(2, 1, 0, 3))


def _fm(v):
    return np.ascontiguousarray(np.asarray(v).reshape(8, 128).T)


def _rope_tables():
    p = np.arange(128)
    dm_ = p % 64
    axis = dm_ // 32
    half = (dm_ % 32) // 16
    f = dm_ % 16
    l = np.arange(LAT)
    pos = np.stack([l // 64, l % 64], 0).astype(np.float32)
    inv = (np.float32(10000.0) ** (-(np.arange(16, dtype=np.float32)) / np.float32(16))).astype(np.float32)
    ang = pos[axis, :] * inv[f][:, None]
    cosT = np.cos(ang).astype(np.float32)
    sinT = np.sin(ang).astype(np.float32)
    sinT = np.where(half[:, None] == 0, -sinT, sinT).astype(np.float32)
    return np.ascontiguousarray(np.stack([cosT, sinT], 0))


def _consts():
    ident = np.eye(128, dtype=np.float32)
    perm = np.zeros((128, 128), np.float32)
    for i in range(128):
        b16 = (i % 32) // 16
        j = i + 16 if b16 == 0 else i - 16
        perm[i, j] = 1.0
    iota = np.tile(np.arange(16, dtype=np.float32)[None, :], (128, 1))
    return np.ascontiguousarray(np.concatenate([ident, perm, iota], 1))


def prep_shared(inp):
    sh = {}
    f = lambda a: np.asarray(a, dtype=np.float32)
    pv = np.zeros((DEPTH, 128, NPV), np.float32)
    for li in range(DEPTH):
        pv[li, :, PV_N1:PV_N1 + 8] = _fm(inp["norm1_g"][li])
        pv[li, :, PV_N2:PV_N2 + 8] = _fm(inp["norm2_g"][li])
        for k in range(4):
            pv[li, :, PV_CW + k * 8:PV_CW + k * 8 + 8] = _fm(inp["conv_w"][li, k])
        pv[li, :, PV_CB:PV_CB + 8] = _fm(inp["conv_b"][li])
        for d_ in range(2):
            for g_ in range(2):
                o = PV_LB + (d_ * 2 + g_) * 8
                pv[li, :, o:o + 8] = _fm(inp["lru_b"][li, d_, g_])
            pv[li, :, PV_LL + d_ * 8:PV_LL + d_ * 8 + 8] = _fm(inp["lru_lam"][li, d_])
        for j in range(6):
            pv[li, :, PV_MB + j * 8:PV_MB + j * 8 + 8] = _fm(inp["mod_b"][li, j * 1024:(j + 1) * 1024])
        pv[li, :, PV_SG] = inp["subln_g"][li]
        pv[li, :, PV_DL:PV_DL + 256] = np.broadcast_to(f(inp["diff_lam"][li]).reshape(1, 256), (128, 256))
    sh["pv"] = pv
    sh["modw"] = np.stack([_lay(f(inp["mod_w"][li])) for li in range(DEPTH)])
    sh["win"] = np.stack([_lay(f(inp["w_in"][li])) for li in range(DEPTH)])
    sh["wbl"] = np.stack([_lay(f(inp["w_br_lru"][li])) for li in range(DEPTH)])
    sh["wba"] = np.stack([_lay(f(inp["w_br_attn"][li])) for li in range(DEPTH)])
    sh["wo"] = np.stack([_lay(f(inp["w_out"][li])) for li in range(DEPTH)])
    sh["wq"] = np.stack([_lay(f(inp["peer_wq"][li])) for li in range(DEPTH)])
    lw = f(inp["lru_w"])
    lruw = np.zeros((DEPTH, 8, 128, 4, 128), np.float32)
    for c in range(8):
        for d_ in range(2):
            for g_ in range(2):
                lruw[:, c, 0:64, d_ * 2 + g_, 0:64] = lw[:, d_, g_, 2 * c]
                lruw[:, c, 64:128, d_ * 2 + g_, 64:128] = lw[:, d_, g_, 2 * c + 1]
    sh["lruw"] = lruw
    pk = f(inp["peer_keys"])
    kb = np.zeros((DEPTH, 128, 8, 256), np.float32)
    for h in range(8):
        kb[:, 0:64, h, 0:128] = pk[:, h, 0].transpose(0, 2, 1)
        kb[:, 64:128, h, 128:256] = pk[:, h, 1].transpose(0, 2, 1)
    sh["keysbd"] = kb
    for li in range(DEPTH):
        sh["pu%d" % li] = np.ascontiguousarray(f(inp["peer_u"][li]))
        sh["pvt%d" % li] = np.ascontiguousarray(f(inp["peer_v"][li]))
    sh["cst"] = _consts()
    sh["ropet"] = _rope_tables()
    return sh


def prep_core(inp, b):
    xcat = np.concatenate([np.asarray(inp["ctx"][b], np.float32), np.asarray(inp["x"][b], np.float32)], 0)
    gv = np.concatenate([_fm(inp["c"][b]), _fm(inp["c_ctx"]), _fm(inp["final_g"])], 1).astype(np.float32)
    return {"xT0": np.ascontiguousarray(xcat.T), "gv": np.ascontiguousarray(gv)}


def kernel(**inputs):
    nb = inputs["x"].shape[0]
    sh = prep_shared(inputs)
    nc = build_program()
    in_maps = []
    for b in range(nb):
        m = dict(sh)
        m.update(prep_core(inputs, b))
        in_maps.append(m)
    res = run_bass_kernel_spmd(nc, in_maps, core_ids=list(range(nb)))
    out = np.stack([np.ascontiguousarray(res.results[b]["outT"].T) for b in range(nb)], 0)
    return out.astype(np.float32)
```

```python
import math
from contextlib import ExitStack

import numpy as np
import concourse.bass as bass
import concourse.mybir as mybir
from concourse.bass_utils import run_bass_kernel_spmd

F32 = mybir.dt.float32
BF16 = mybir.dt.bfloat16
U32 = mybir.dt.uint32
AF = mybir.ActivationFunctionType
ALU = mybir.AluOpType
AX = mybir.AxisListType

ENGS = ["sync", "scalar", "vector", "gpsimd", "tensor"]

D = 1024
NCH = 8
CTX = 256
LAT = 2048
T = CTX + LAT
NT = T // 128
DEPTH = 4
EPS = 1e-6
BLOCKS = [(0, 256)] + [(256 + 512 * i, 512) for i in range(4)]
NEXP = 16384

PV_N1 = 0
PV_N2 = 8
PV_CW = 16
PV_CB = 48
PV_LB = 56
PV_LL = 88
PV_MB = 104
PV_SG = 152
PV_DL = 153
NPV = 409

DEBUG = {}


class Prog:
    def __init__(self, nc, n_dma_slots=16):
        self.nc = nc
        self.ops = {e: [] for e in ENGS}
        self.cnt = {e: 0 for e in ENGS}
        self.last_w = {}
        self.readers = {}
        self.seen = {e: {} for e in ENGS}
        self.nslots = n_dma_slots
        self.dma_i = {e: 0 for e in ENGS}
        self.dma_last = {}

    def _deps(self, eng, reads, writes):
        deps = {}

        def add(tok):
            if tok is None:
                return
            k, v = tok
            if deps.get(k, 0) < v:
                deps[k] = v
        for r in reads:
            add(self.last_w.get(r))
        for w in writes:
            add(self.last_w.get(w))
            for t in self.readers.get(w, ()):
                add(t)
        waits = []
        for k, v in deps.items():
            if eng == "tensor" and k == ("eng", "tensor"):
                continue
            if self.seen[eng].get(k, 0) >= v:
                continue
            self.seen[eng][k] = v
            waits.append((k, v))
        return waits

    def _commit(self, tok, reads, writes):
        for w in writes:
            self.last_w[w] = tok
            self.readers[w] = []
        for r in reads:
            if r in writes:
                continue
            self.readers.setdefault(r, []).append(tok)

    def op(self, eng, fn, reads=(), writes=()):
        reads = list(reads)
        writes = list(writes)
        waits = self._deps(eng, reads, writes)
        self.cnt[eng] += 1
        tok = (("eng", eng), self.cnt[eng])
        self.ops[eng].append((fn, waits, ("eng", eng), 1))
        self._commit(tok, reads, writes)
        return tok

    def dma(self, eng, fn, reads=(), writes=()):
        reads = list(reads)
        writes = list(writes)
        i = self.dma_i[eng]
        self.dma_i[eng] += 1
        slot = i % self.nslots
        val = 16 * (i // self.nslots + 1)
        key = ("dma", eng, slot)
        waits = self._deps(eng, reads, writes)
        if val > 16:
            prev = val - 16
            if self.seen[eng].get(key, 0) < prev:
                self.seen[eng][key] = prev
                waits.append((key, prev))
        self.ops[eng].append((fn, waits, key, 16))
        tok = (key, val)
        self.dma_last[key] = val
        self._commit(tok, reads, writes)
        return tok

    def barrier(self):
        toks = [(("eng", e), self.cnt[e]) for e in ENGS if self.cnt[e] > 0]
        toks += list(self.dma_last.items())
        for e in ENGS:
            waits = []
            for k, v in toks:
                if self.seen[e].get(k, 0) >= v:
                    continue
                self.seen[e][k] = v
                waits.append((k, v))
            if waits:
                self.ops[e].append((None, waits, None, 0))

    def emit(self):
        nc = self.nc
        self.barrier()
        keys = set()
        for e in ENGS:
            for (_, waits, ik, _) in self.ops[e]:
                if ik is not None:
                    keys.add(ik)
                for k, _ in waits:
                    keys.add(k)
        with ExitStack() as st:
            sems = {}
            for k in sorted(keys, key=str):
                nm = "s_" + "_".join(str(x) for x in k)
                sems[k] = st.enter_context(nc.semaphore(nm))
            block = st.enter_context(nc.Block())

            def mk(e):
                def body(eng):
                    for (fn, waits, ik, inc) in self.ops[e]:
                        for k, v in waits:
                            eng.wait_ge(sems[k], v)
                        if fn is not None:
                            ins = fn(eng)
                            ins.then_inc(sems[ik], inc)
                return body
            for e in ENGS:
                getattr(block, e)(mk(e))


def mm(out, lhsT, rhs, start, stop):
    return lambda e: e.matmul(out, lhsT=lhsT, rhs=rhs, start=start, stop=stop)


def tr(out, in_, ident):
    return lambda e: e.transpose(out, in_, ident)


def act(out, in_, func, **kw):
    return lambda e: e.activation(out=out, in_=in_, func=func, **kw)


def tt(out, in0, in1, op):
    return lambda e: e.tensor_tensor(out=out, in0=in0, in1=in1, op=op)


def ts(out, in0, s1, s2, op0, op1=None):
    if op1 is None:
        return lambda e: e.tensor_scalar(out=out, in0=in0, scalar1=s1, scalar2=None, op0=op0)
    return lambda e: e.tensor_scalar(out=out, in0=in0, scalar1=s1, scalar2=s2, op0=op0, op1=op1)


def stt(out, in0, scalar, in1, op0, op1):
    return lambda e: e.scalar_tensor_tensor(out=out, in0=in0, scalar=scalar, in1=in1, op0=op0, op1=op1)


def cp(out, in_):
    return lambda e: e.tensor_copy(out=out, in_=in_)


def rcp(out, in_):
    return lambda e: e.reciprocal(out=out, in_=in_)


def red(out, in_, op):
    return lambda e: e.tensor_reduce(out=out, in_=in_, axis=AX.X, op=op)


def dm(out, in_):
    return lambda e: e.dma_start(out=out, in_=in_)


_BREG = {}


def gather(out, tab, idx):
    def f(e):
        if "r" not in _BREG:
            _BREG["r"] = e.to_reg(NEXP - 1)
        return e.indirect_dma_start(
            out=out, out_offset=None, in_=tab,
            in_offset=bass.IndirectOffsetOnAxis(ap=idx, axis=0),
            bounds_check=_BREG["r"], oob_is_err=False)
    return f


def build_program(n_layers=DEPTH, stop_phase=None, peer_tiles=None):
    nc = bass.Bass("TRN2", target_bir_lowering=False)
    _BREG.clear()

    def din(name, shape, dt=F32):
        return nc.dram_tensor(name, list(shape), dt, kind="ExternalInput").ap()

    xT0 = din("xT0", [D, T])
    gv = din("gv", [128, 24])
    pvd = din("pv", [DEPTH, 128, NPV])
    modw = din("modw", [DEPTH, 48, 128, 8, 128])
    win = din("win", [DEPTH, 56, 128, 8, 128])
    wbl = din("wbl", [DEPTH, 8, 128, 8, 128])
    wba = din("wba", [DEPTH, 8, 128, 8, 128])
    wo = din("wo", [DEPTH, 8, 128, 8, 128])
    wq = din("wq", [DEPTH, 8, 128, 8, 128])
    lruw = din("lruw", [DEPTH, 8, 128, 4, 128])
    keysbd = din("keysbd", [DEPTH, 128, 8, 256])
    ntab = 8 if DEBUG.get("no_peer_tables") else NEXP
    pu = [din("pu%d" % i, [ntab, D]) for i in range(DEPTH)]
    pvt_ = [din("pvt%d" % i, [ntab, D]) for i in range(DEPTH)]
    cst = din("cst", [128, 128 * 2 + 16])
    ropet = din("ropet", [2, 128, LAT])
    outT = nc.dram_tensor("outT", [D, LAT], F32, kind="ExternalOutput").ap()
    dbg = nc.dram_tensor("dbg", [D, T], F32, kind="ExternalOutput").ap() if DEBUG.get("dump") else None
    pub = [nc.dram_tensor("pub%d" % i, [NEXP, D], BF16, kind="Internal").ap() for i in range(DEPTH)]
    pvb = [nc.dram_tensor("pvb%d" % i, [NEXP, D], BF16, kind="Internal").ap() for i in range(DEPTH)]
    zTd = nc.dram_tensor("zTd", [8, 128, T], BF16, kind="Internal").ap()
    oTd = nc.dram_tensor("oTd", [8, 128, T], BF16, kind="Internal").ap()

    ARB = 108544
    with ExitStack() as st:
        def sb(name, shape, dt=F32):
            return st.enter_context(nc.sbuf_tensor(name, list(shape), dt))

        xT = sb("xT", [128, NCH, T])
        arena = sb("arena", [128, ARB // 4])
        cs_t = sb("cs_t", [128, 272])
        identb = sb("identb", [128, 128], BF16)
        permb = sb("permb", [128, 128], BF16)
        onesf = sb("onesf", [128, 128])
        onesb = sb("onesb", [128, 128], BF16)
        pvt = sb("pvt", [128, NPV])
        gvt = sb("gvt", [128, 24])
        sil = sb("sil", [128, 8, 2])
        mT = sb("mT", [128, 48, 2])
        A1 = sb("A1", [128, 8, 2])
        A2 = sb("A2", [128, 8, 2])
        sm = sb("sm", [128, 256])
        lamt = sb("lamt", [128, 4])
        csv = sb("csv", [128, 32])
        sgt = sb("sgt", [128, 1])
        NWB = 6
        wb = [sb("wb%d" % i, [128, 8, 128], BF16) for i in range(NWB)]
        lwb = [sb("lwb%d" % i, [128, 4, 128], BF16) for i in range(2)]
        pb = [st.enter_context(nc.psum_tensor("pb%d" % i, [128, 512], F32)) for i in range(8)]

        ident = cs_t[:, 0:128]
        permf = cs_t[:, 128:256]
        iota16 = cs_t[:, 256:272]

        def av(off, dt, n, shape=None):
            esz = 4 if dt in (F32, U32) else 2
            assert off % 4 == 0 and (n * esz) % 4 == 0 and off + n * esz <= ARB
            v = arena[:, off // 4: (off + n * esz) // 4]
            if dt != F32:
                v = v.bitcast(dt)
            if shape is not None:
                names = " ".join("d%d" % i for i in range(len(shape)))
                kw = {"d%d" % i: s for i, s in enumerate(shape)}
                v = v.rearrange("p (%s) -> p %s" % (names, names), **kw)
            return v

        P = Prog(nc)
        wbi = [0]

        def next_wb():
            i = wbi[0] % NWB
            wbi[0] += 1
            return i

        def load_w(src):
            i = next_wb()
            P.dma("gpsimd", dm(wb[i][:], src), reads=[], writes=[("wb", i)])
            return i

        P.dma("sync", dm(cs_t[:], cst), writes=["cst"])
        P.dma("sync", dm(gvt[:], gv), writes=["gvt"])
        for c in range(NCH):
            P.dma("sync", dm(xT[:, c, :], xT0[c * 128:(c + 1) * 128, :]),
                  writes=[("xT", c, b) for b in range(5)])
        P.op("vector", cp(identb[:], ident), reads=["cst"], writes=["identb"])
        P.op("vector", cp(permb[:], permf), reads=["cst"], writes=["permb"])
        P.op("gpsimd", lambda e: e.memset(onesf[:], 1.0), writes=["onesf"])
        P.op("gpsimd", lambda e: e.memset(onesb[:], 1.0), writes=["onesb"])
        P.op("scalar", act(sil[:, :, 0], gvt[:, 0:8], AF.Silu), reads=["gvt"], writes=["sil"])
        P.op("scalar", act(sil[:, :, 1], gvt[:, 8:16], AF.Silu), reads=["gvt", "sil"], writes=["sil"])
        if not DEBUG.get("no_peer_tables"):
            stage = [av(i * 16384, BF16, 8 * 1024, [8, 1024]) for i in range(2)]
            k_ = 0
            for li in range(n_layers):
                for (src, dst, nm) in ((pu[li], pub[li], "pub"), (pvt_[li], pvb[li], "pvb")):
                    srcv = src.rearrange("(c p i) d -> c p i d", p=128, i=8)
                    dstv = dst.rearrange("(c p i) d -> c p i d", p=128, i=8)
                    for c in range(16):
                        b_ = k_ % 2
                        k_ += 1
                        for i in range(8):
                            P.dma("gpsimd", dm(stage[b_][:, i, :], srcv[c, :, i, :]), writes=[("stg", b_, i)])
                        P.dma("sync", dm(dstv[c], stage[b_][:]), reads=[("stg", b_, i) for i in range(8)],
                              writes=[(nm, li, c)])
        P.barrier()

        def norm_mod(blk_s0, n, s, Ax, shift_j, dst_fn, key_x, key_dst, sq_off, pbank, bw=512):
            sq = [av(sq_off + i * bw * 4, F32, bw) for i in range(2)]
            r1 = av(sq_off + 2 * bw * 4, F32, bw)
            rstd = av(sq_off + 3 * bw * 4, F32, bw)
            tmp = [av(sq_off + (4 + i) * bw * 4, F32, bw) for i in range(2)]
            for c in range(NCH):
                P.op("scalar", act(sq[c % 2][:, :n], xT[:, c, blk_s0:blk_s0 + n], AF.Square),
                     reads=[key_x(c)], writes=[("sq", c % 2)])
                P.op("tensor", mm(pb[pbank][:, :n], onesf[:], sq[c % 2][:, :n], c == 0, c == NCH - 1),
                     reads=[("sq", c % 2), "onesf"], writes=[("pb", pbank)])
            P.op("scalar", act(r1[:, :n], pb[pbank][:, :n], AF.Sqrt, scale=1.0 / D, bias=epst[:, 0:1]),
                 reads=[("pb", pbank), "epst"], writes=["r1"])
            P.op("vector", rcp(rstd[:, :n], r1[:, :n]), reads=["r1"], writes=["rstd"])
            for c in range(NCH):
                P.op("vector", tt(tmp[c % 2][:, :n], xT[:, c, blk_s0:blk_s0 + n], rstd[:, :n], ALU.mult),
                     reads=[key_x(c), "rstd"], writes=[("ntmp", c % 2)])
                P.op("scalar", act(dst_fn(c), tmp[c % 2][:, :n], AF.Identity,
                                   scale=Ax[:, c, s:s + 1], bias=mT[:, shift_j * 8 + c, s:s + 1]),
                     reads=[("ntmp", c % 2), "mT", "Ax"], writes=[key_dst(c)])

        epst = sb("epst", [128, 2])
        P.op("gpsimd", lambda e: e.memset(epst[:, 0:1], EPS), writes=["epst"])
        P.op("gpsimd", lambda e: e.memset(epst[:, 1:2], 1.0), writes=["epst"])

        hT = av(0, BF16, NCH * T, [NCH, T])
        HB = 36864

        for li in range(n_layers):
            last = (li == DEPTH - 1)
            lambda_init = 0.8 - 0.6 * math.exp(-0.3 * li)
            blocks = [(i, s0, n) for i, (s0, n) in enumerate(BLOCKS)]
            oblocks = [b for b in blocks if not (last and b[0] == 0)]

            P.dma("sync", dm(pvt[:], pvd[li]), writes=["pvt"])
            wst = [av(HB + i * 4096, F32, 1024, [8, 128]) for i in range(2)]
            psm = pb[0][:, 0:96]
            for jc in range(48):
                P.dma("sync", dm(wst[jc % 2], modw[li, jc]), writes=[("wst", jc % 2)])
                for kc in range(8):
                    P.op("tensor", mm(psm[:, jc * 2:(jc + 1) * 2], wst[jc % 2][:, kc, :], sil[:, kc, :], kc == 0, kc == 7),
                         reads=[("wst", jc % 2), "sil"], writes=[("pb", 0)])
            P.op("vector", tt(mT[:], psm.rearrange("p (a b) -> p a b", b=2),
                              pvt[:, PV_MB:PV_MB + 48].unsqueeze(2).to_broadcast([128, 48, 2]), ALU.add),
                 reads=[("pb", 0), "pvt"], writes=["mT"])
            for (Ax, j, pg, nm) in ((A1, 1, PV_N1, "A1"), (A2, 4, PV_N2, "A2")):
                P.op("vector", ts(Ax[:], mT[:, j * 8:(j + 1) * 8, :], 1.0, None, ALU.add), reads=["mT"], writes=[nm])
                P.op("vector", tt(Ax[:], Ax[:], pvt[:, pg:pg + 8].unsqueeze(2).to_broadcast([128, 8, 2]), ALU.mult),
                     reads=[nm, "pvt"], writes=[nm])
            dl = pvt[:, PV_DL:PV_DL + 256]
            P.op("vector", tt(sm[:, 0:64], dl[:, 0:64], dl[:, 64:128], ALU.mult), reads=["pvt"], writes=["sm"])
            P.op("vector", tt(sm[:, 64:128], dl[:, 128:192], dl[:, 192:256], ALU.mult), reads=["pvt", "sm"], writes=["sm"])
            P.op("vector", red(sm[:, 128:130], sm[:, 0:128].rearrange("p (a b) -> p a b", b=64), ALU.add),
                 reads=["sm"], writes=["sm"])
            P.op("scalar", act(sm[:, 130:132], sm[:, 128:130], AF.Exp), reads=["sm"], writes=["sm"])
            P.op("vector", ts(lamt[:, 0:1], sm[:, 130:131], sm[:, 131:132], float(lambda_init), ALU.subtract, ALU.add),
                 reads=["sm"], writes=["lamt"])
            P.op("vector", ts(lamt[:, 1:2], lamt[:, 0:1], -1.0, None, ALU.mult), reads=["lamt"], writes=["lamt"])
            P.op("vector", ts(sgt[:], pvt[:, PV_SG:PV_SG + 1], float(1.0 - lambda_init), None, ALU.mult),
                 reads=["pvt"], writes=["sgt"])
            P.op("scalar", act(sm[:, 136:152], pvt[:, PV_LL:PV_LL + 16], AF.Sigmoid), reads=["pvt", "sm"], writes=["sm"])
            P.op("scalar", act(sm[:, 136:152], sm[:, 136:152], AF.Ln), reads=["sm"], writes=["sm"])
            P.op("vector", ts(csv[:, 0:16], sm[:, 136:152], 8.0, None, ALU.mult), reads=["sm"], writes=["csv"])
            P.op("vector", ts(csv[:, 16:32], sm[:, 136:152], 16.0, None, ALU.mult), reads=["sm", "csv"], writes=["csv"])
            P.barrier()

            for (bi, s0, n) in blocks:
                s = 1 if bi == 0 else 0
                norm_mod(s0, n, s, A1, 0,
                         lambda c, s0=s0, n=n: hT[:, c, s0:s0 + n],
                         lambda c, bi=bi: ("xT", c, bi), lambda c, bi=bi: ("hT", c, bi),
                         HB, bi % 2)
            P.barrier()

            def inproj_fm(widx, bi, s0, n, pbank):
                for kc in range(8):
                    P.op("tensor", mm(pb[pbank][:, :n], wb[widx][:, kc, :], hT[:, kc, s0:s0 + n], kc == 0, kc == 7),
                         reads=[("wb", widx), ("hT", kc, bi)], writes=[("pb", pbank)])

            if stop_phase != "A" and not DEBUG.get("skip_b"):
                uc = av(HB, F32, T)
                Ab = av(HB + 9216, F32, T)
                Bb = av(HB + 18432, F32, T)
                C0 = av(HB + 27648, F32, T)
                C1 = av(HB + 36864, F32, T)
                ucb = av(HB + 46080, BF16, T)
                zc = av(HB + 50688, BF16, T)
                for c in range(NCH):
                    wu = load_w(win[li, 0 + c])
                    wg = load_w(win[li, 8 + c])
                    lw = c % 2
                    P.dma("gpsimd", dm(lwb[lw][:], lruw[li, c]), writes=[("lwb", lw)])
                    for (bi, s0, n) in blocks:
                        inproj_fm(wu, bi, s0, n, bi % 2)
                        P.op("scalar", act(C1[:, s0:s0 + n], pb[bi % 2][:, :n], AF.Identity),
                             reads=[("pb", bi % 2)], writes=[("C1", bi)])
                    allb = range(5)
                    cw = lambda k: pvt[:, PV_CW + k * 8 + c: PV_CW + k * 8 + c + 1]
                    for (a, b, bl) in ((0, CTX, [0]), (CTX, T, [1, 2, 3, 4])):
                        rk = [("C1", i) for i in bl]
                        wk_ = [("uc", i) for i in bl]
                        P.op("vector", ts(uc[:, a:b], C1[:, a:b], cw(2), pvt[:, PV_CB + c:PV_CB + c + 1], ALU.mult, ALU.add),
                             reads=rk + ["pvt"], writes=wk_)
                        P.op("vector", stt(uc[:, a + 2:b], C1[:, a:b - 2], cw(0), uc[:, a + 2:b], ALU.mult, ALU.add),
                             reads=rk + ["pvt"] + wk_, writes=wk_)
                        P.op("vector", stt(uc[:, a + 1:b], C1[:, a:b - 1], cw(1), uc[:, a + 1:b], ALU.mult, ALU.add),
                             reads=rk + ["pvt"] + wk_, writes=wk_)
                        P.op("vector", stt(uc[:, a:b - 1], C1[:, a + 1:b], cw(3), uc[:, a:b - 1], ALU.mult, ALU.add),
                             reads=rk + ["pvt"] + wk_, writes=wk_)
                    ucK = [("uc", i) for i in allb]
                    P.op("gpsimd", cp(ucb[:], uc[:]), reads=ucK, writes=["ucb"])
                    for d_ in range(2):
                        Cd = C0 if d_ == 0 else C1
                        CdK = [("C0" if d_ == 0 else "C1", i) for i in allb]
                        for (bi, s0, n) in blocks:
                            for g_, (dst, nm) in enumerate(((Ab, "Ab"), (Bb, "Bb"))):
                                pbk = 2 + g_ * 2 + (bi % 2)
                                P.op("tensor", mm(pb[pbk][:, :n], lwb[lw][:, d_ * 2 + g_, :], ucb[:, s0:s0 + n], True, True),
                                     reads=[("lwb", lw), "ucb"], writes=[("pb", pbk)])
                                bcol = PV_LB + (d_ * 2 + g_) * 8 + c
                                P.op("scalar", act(dst[:, s0:s0 + n], pb[pbk][:, :n], AF.Sigmoid, bias=pvt[:, bcol:bcol + 1]),
                                     reads=[("pb", pbk), "pvt"], writes=[(nm, bi)])
                        AbK = [("Ab", i) for i in allb]
                        BbK = [("Bb", i) for i in allb]
                        csc = csv[:, d_ * 8 + c: d_ * 8 + c + 1]
                        cs2c = csv[:, 16 + d_ * 8 + c: 16 + d_ * 8 + c + 1]
                        P.op("scalar", act(Cd[:], Ab[:], AF.Exp, scale=cs2c), reads=AbK + ["csv"], writes=CdK)
                        P.op("scalar", act(Ab[:], Ab[:], AF.Exp, scale=csc), reads=AbK + ["csv"], writes=AbK)
                        P.op("scalar", act(Cd[:], Cd[:], AF.Sqrt, scale=-1.0, bias=epst[:, 1:2]), reads=CdK + ["epst"], writes=CdK)
                        P.op("gpsimd", tt(Bb[:], Bb[:], uc[:], ALU.mult), reads=BbK + ucK, writes=BbK)
                        P.op("vector", tt(Bb[:], Bb[:], Cd[:], ALU.mult), reads=BbK + CdK, writes=BbK)
                        if d_ == 0:
                            P.op("vector", lambda e, Cd=Cd: e.tensor_tensor_scan(
                                out=Cd[:], data0=Ab[:], data1=Bb[:], initial=0.0, op0=ALU.mult, op1=ALU.add),
                                reads=AbK + BbK, writes=CdK)
                        else:
                            P.op("vector", lambda e, Cd=Cd: e.tensor_tensor_scan(
                                out=Cd[:, 0:CTX][:, ::-1], data0=Ab[:, 0:CTX][:, ::-1], data1=Bb[:, 0:CTX][:, ::-1],
                                initial=0.0, op0=ALU.mult, op1=ALU.add),
                                reads=AbK + BbK, writes=CdK)
                            P.op("vector", lambda e, Cd=Cd: e.tensor_tensor_scan(
                                out=Cd[:, CTX:T][:, ::-1], data0=Ab[:, CTX:T][:, ::-1], data1=Bb[:, CTX:T][:, ::-1],
                                initial=Cd[:, 0:1], op0=ALU.mult, op1=ALU.add),
                                reads=AbK + BbK + CdK, writes=CdK)
                    C0K = [("C0", i) for i in allb]
                    C1K = [("C1", i) for i in allb]
                    P.op("gpsimd", tt(C0[:], C0[:], C1[:], ALU.add), reads=C0K + C1K, writes=C0K)
                    for (bi, s0, n) in blocks:
                        inproj_fm(wg, bi, s0, n, bi % 2)
                        P.op("scalar", act(Ab[:, s0:s0 + n], pb[bi % 2][:, :n], AF.Gelu_apprx_tanh),
                             reads=[("pb", bi % 2)], writes=[("Ab", bi)])
                    P.op("vector", tt(zc[:], C0[:], Ab[:], ALU.mult), reads=C0K + [("Ab", i) for i in allb], writes=["zc"])
                    P.dma("sync", dm(zTd[c], zc[:]), reads=["zc"], writes=[("zTd", c)])
                P.barrier()

            if stop_phase not in ("A", "B"):
                cosT = av(HB, F32, LAT)
                sinT = av(HB + 8192, F32, LAT)
                qTt = av(HB + 16384, BF16, T)
                kTt = av(HB + 20992, BF16, T)
                vh = av(HB + 25600, BF16, NT * 128, [NT, 128])
                qraw = av(HB + 30208, BF16, 512)
                pT = [av(HB + 31232 + i * 1024, BF16, 512) for i in range(3)]
                wk = [av(HB + 34304 + i * 2048, F32, 512) for i in range(6)]
                oTc = av(HB + 46592, BF16, T)
                P.dma("sync", dm(cosT, ropet[0]), writes=["cosT"])
                P.dma("sync", dm(sinT, ropet[1]), writes=["sinT"])
                pti = [0]
                for h in range(8):
                    wqi = load_w(win[li, 16 + h])
                    wki = load_w(win[li, 24 + h])
                    wvi = load_w(win[li, 32 + h])
                    for (widx, dstT, nm) in ((wqi, qTt, "qT"), (wki, kTt, "kT")):
                        for (bi, s0, n) in (blocks if not DEBUG.get("skip_qk") else []):
                            if nm == "qT" and last and bi == 0:
                                continue
                            inproj_fm(widx, bi, s0, n, 7)
                            if bi == 0 or DEBUG.get("norope"):
                                P.op("scalar", act(dstT[:, s0:s0 + n], pb[7][:, :n], AF.Identity),
                                     reads=[("pb", 7)], writes=[(nm, bi)])
                            else:
                                l0 = s0 - CTX
                                P.op("scalar", act(qraw[:, :n], pb[7][:, :n], AF.Identity), reads=[("pb", 7)], writes=["qraw"])
                                P.op("tensor", mm(pb[6][:, :n], permb[:], qraw[:, :n], True, True),
                                     reads=["qraw", "permb"], writes=[("pb", 6)])
                                P.op("scalar", act(wk[2][:, :n], pb[7][:, :n], AF.Identity), reads=[("pb", 7)], writes=[("wk", 2)])
                                P.op("scalar", act(wk[3][:, :n], pb[6][:, :n], AF.Identity), reads=[("pb", 6)], writes=[("wk", 3)])
                                P.op("vector", tt(wk[0][:, :n], wk[2][:, :n], cosT[:, l0:l0 + n], ALU.mult),
                                     reads=[("wk", 2), "cosT"], writes=[("wk", 0)])
                                P.op("gpsimd", tt(wk[1][:, :n], wk[3][:, :n], sinT[:, l0:l0 + n], ALU.mult),
                                     reads=[("wk", 3), "sinT"], writes=[("wk", 1)])
                                P.op("vector", tt(dstT[:, s0:s0 + n], wk[0][:, :n], wk[1][:, :n], ALU.add),
                                     reads=[("wk", 0), ("wk", 1)], writes=[(nm, bi)])
                    for t_ in (range(NT) if not DEBUG.get("skip_v") else []):
                        bi = 0 if t_ < 2 else 1 + (t_ - 2) // 4
                        pbk = 6 + (t_ % 2)
                        for kc in range(8):
                            P.op("tensor", mm(pb[pbk][:, 0:128], hT[:, kc, t_ * 128:(t_ + 1) * 128], wb[wvi][:, kc, :], kc == 0, kc == 7),
                                 reads=[("wb", wvi), ("hT", kc, bi)], writes=[("pb", pbk)])
                        P.op("scalar", act(vh[:, t_, :], pb[pbk][:, 0:128], AF.Identity), reads=[("pb", pbk)], writes=[("vh", t_)])
                    for (bi, s0, n) in (oblocks if DEBUG.get("c_sub", 3) >= 2 else []):
                        ktiles = [0, 1] if bi == 0 else list(range(NT))
                        for m_ in range(2):
                            pso, psz = 2 + 2 * m_, 3 + 2 * m_
                            for ki, kt in enumerate(ktiles):
                                kbi = 0 if kt < 2 else 1 + (kt - 2) // 4
                                j = pti[0] % 2
                                j2 = pti[0] % 3
                                pti[0] += 1
                                P.op("tensor", mm(pb[j][:, :n], kTt[m_ * 64:(m_ + 1) * 64, kt * 128:(kt + 1) * 128],
                                                  qTt[m_ * 64:(m_ + 1) * 64, s0:s0 + n], True, True),
                                     reads=[("kT", kbi), ("qT", bi)], writes=[("pb", j)])
                                P.op("scalar", act(pT[j2][:, :n], pb[j][:, :n], AF.Exp, scale=0.125),
                                     reads=[("pb", j)], writes=[("pT", j2)])
                                P.op("tensor", mm(pb[pso][:, :n], vh[:, kt, :], pT[j2][:, :n], ki == 0, ki == len(ktiles) - 1),
                                     reads=[("vh", kt), ("pT", j2)], writes=[("pb", pso)])
                                P.op("tensor", mm(pb[psz][:, :n], onesb[:], pT[j2][:, :n], ki == 0, ki == len(ktiles) - 1),
                                     reads=["onesb", ("pT", j2)], writes=[("pb", psz)])
                        if DEBUG.get("c_sub", 3) < 3:
                            continue
                        W = lambda i: wk[i][:, :n]
                        P.op("vector", rcp(W(0), pb[3][:, :n]), reads=[("pb", 3)], writes=[("wk", 0)])
                        P.op("vector", tt(W(1), pb[2][:, :n], W(0), ALU.mult), reads=[("pb", 2), ("wk", 0)], writes=[("wk", 1)])
                        P.op("vector", rcp(W(2), pb[5][:, :n]), reads=[("pb", 5)], writes=[("wk", 2)])
                        P.op("vector", tt(W(3), pb[4][:, :n], W(2), ALU.mult), reads=[("pb", 4), ("wk", 2)], writes=[("wk", 3)])
                        P.op("vector", stt(W(4), W(3), lamt[:, 1:2], W(1), ALU.mult, ALU.add),
                             reads=[("wk", 3), ("wk", 1), "lamt"], writes=[("wk", 4)])
                        P.op("scalar", act(W(5), W(4), AF.Square), reads=[("wk", 4)], writes=[("wk", 5)])
                        P.op("tensor", mm(pb[6][:, :n], onesf[:], W(5), True, True), reads=["onesf", ("wk", 5)], writes=[("pb", 6)])
                        P.op("scalar", act(W(0), pb[6][:, :n], AF.Sqrt, scale=1.0 / 128, bias=epst[:, 0:1]),
                             reads=[("pb", 6), "epst"], writes=[("wk", 0)])
                        P.op("vector", rcp(W(2), W(0)), reads=[("wk", 0)], writes=[("wk", 2)])
                        P.op("vector", tt(W(1), W(4), W(2), ALU.mult), reads=[("wk", 4), ("wk", 2)], writes=[("wk", 1)])
                        P.op("scalar", act(oTc[:, s0:s0 + n], W(1), AF.Identity, scale=sgt[:, 0:1]),
                             reads=[("wk", 1), "sgt"], writes=[("oTc", bi)])
                    P.dma("sync", dm(oTd[h], oTc[:]), reads=[("oTc", i) for i in range(5)], writes=[("oTd", h)])
                P.barrier()

            if stop_phase not in ("A", "B", "C"):
                zb = av(HB, BF16, 8 * 512, [8, 512])
                ob = av(HB + 8192, BF16, 8 * 512, [8, 512])
                merged = av(HB + 16384, BF16, NCH * T, [NCH, T])
                wk = [av(HB + 53248 + i * 2048, F32, 512) for i in range(4)]
                for m_ in range(8):
                    wA = load_w(wbl[li, m_])
                    wB = load_w(wba[li, m_])
                    wGa = load_w(win[li, 40 + m_])
                    wGb = load_w(win[li, 48 + m_])
                    for (bi, s0, n) in oblocks:
                        P.dma("sync", dm(zb[:, :, :n], zTd[:, :, s0:s0 + n].rearrange("c p t -> p c t")),
                              reads=[("zTd", c) for c in range(8)], writes=["zb"])
                        P.dma("sync", dm(ob[:, :, :n], oTd[:, :, s0:s0 + n].rearrange("c p t -> p c t")),
                              reads=[("oTd", c) for c in range(8)], writes=["ob"])
                        for kc in range(8):
                            P.op("tensor", mm(pb[0][:, :n], wb[wA][:, kc, :], zb[:, kc, :n], kc == 0, kc == 7),
                                 reads=[("wb", wA), "zb"], writes=[("pb", 0)])
                        for kc in range(8):
                            P.op("tensor", mm(pb[1][:, :n], wb[wB][:, kc, :], ob[:, kc, :n], kc == 0, kc == 7),
                                 reads=[("wb", wB), "ob"], writes=[("pb", 1)])
                        inproj_fm(wGa, bi, s0, n, 2)
                        inproj_fm(wGb, bi, s0, n, 3)
                        W = lambda i: wk[i][:, :n]
                        P.op("scalar", act(W(0), pb[2][:, :n], AF.Sigmoid), reads=[("pb", 2)], writes=[("wk", 0)])
                        P.op("scalar", act(W(1), pb[3][:, :n], AF.Sigmoid), reads=[("pb", 3)], writes=[("wk", 1)])
                        P.op("vector", tt(W(2), W(0), pb[0][:, :n], ALU.mult), reads=[("pb", 0), ("wk", 0)], writes=[("wk", 2)])
                        P.op("vector", tt(W(3), W(1), pb[1][:, :n], ALU.mult), reads=[("pb", 1), ("wk", 1)], writes=[("wk", 3)])
                        P.op("gpsimd", tt(merged[:, m_, s0:s0 + n], W(2), W(3), ALU.add),
                             reads=[("wk", 2), ("wk", 3)], writes=[("mg", m_, bi)])
                for m_ in range(8):
                    wO = load_w(wo[li, m_])
                    for (bi, s0, n) in oblocks:
                        s = 1 if bi == 0 else 0
                        pbk = 4 + bi % 2
                        for kc in range(8):
                            P.op("tensor", mm(pb[pbk][:, :n], wb[wO][:, kc, :], merged[:, kc, s0:s0 + n], kc == 0, kc == 7),
                                 reads=[("wb", wO), ("mg", kc, bi)], writes=[("pb", pbk)])
                        P.op("vector", stt(xT[:, m_, s0:s0 + n], pb[pbk][:, :n], mT[:, 16 + m_, s:s + 1], xT[:, m_, s0:s0 + n],
                                           ALU.mult, ALU.add),
                             reads=[("pb", pbk), "mT", ("xT", m_, bi)], writes=[("xT", m_, bi)])
                P.barrier()

            if stop_phase not in ("A", "B", "C", "D"):
                wqb = av(0, BF16, 8 * 8 * 128, [8, 8, 128])
                keysf = av(16384, F32, 8 * 256, [8, 256])
                hn2T = av(24576, BF16, 8 * 128, [8, 128])
                qpT = av(26624, F32, 8 * 128, [8, 128])
                ytok = av(30720, F32, 1024)
                sc = av(34816, F32, 2048)
                sc2 = av(43008, F32, 2048)
                tmpb = av(51200, F32, 2048)
                SO = 59392
                v16 = av(SO, F32, 256, [16, 16])
                i16 = av(SO + 1024, U32, 256, [16, 16])
                i16f = av(SO + 2048, F32, 256, [16, 16])
                best = av(SO + 3072, F32, 128, [8, 16])
                posu = av(SO + 3584, U32, 128)
                k1u = av(SO + 4096, U32, 128)
                k2u = av(SO + 4608, U32, 128)
                k1f = av(SO + 5120, F32, 128, [8, 16])
                k2f = av(SO + 5632, F32, 128, [8, 16])
                i1s = av(SO + 6144, F32, 128)
                i2s = av(SO + 6656, F32, 128)
                idxf = av(SO + 7168, F32, 128)
                idxu = av(SO + 7680, U32, 128)
                gt = av(SO + 8192, F32, 128, [8, 16])
                gs = av(SO + 8704, F32, 8)
                actv = av(SO + 8768, F32, 128)
                coef = av(SO + 9280, F32, 128)
                NDG = 4
                diag = [av(SO + 9792 + i * 256, BF16, 128) for i in range(NDG)]
                junk = av(SO + 10816, BF16, 1024)
                GB = SO + 12864
                NG = 8
                Ug = [av(GB + i * 2048, BF16, 1024) for i in range(NG)]
                Vg = [av(GB + NG * 2048 + i * 2048, BF16, 1024) for i in range(NG)]
                sqo = 24576
                for m_ in range(8):
                    P.dma("gpsimd", dm(wqb[:, m_, :, :], wq[li, m_]), writes=[("wqb", m_)])
                P.dma("sync", dm(keysf, keysbd[li]), writes=["keysf"])
                tiles = list(range(NT))
                if last:
                    tiles = tiles[2:]
                if peer_tiles is not None:
                    tiles = [t_ for t_ in tiles if t_ in peer_tiles]
                for t_ in tiles:
                    s = 1 if t_ < 2 else 0
                    t0 = t_ * 128
                    norm_mod(t0, 128, s, A2, 3,
                             lambda c: hn2T[:, c, :],
                             lambda c, t_=t_: ("xT", c, t_), lambda c: ("hn2T", c),
                             105024, 0, bw=128)
                    for hh in range(8):
                        pbk = 1 + hh // 4
                        for kc in range(8):
                            P.op("tensor", mm(pb[pbk][:, (hh % 4) * 128:(hh % 4 + 1) * 128], wqb[:, hh, kc, :], hn2T[:, kc, :], kc == 0, kc == 7),
                                 reads=[("wqb", hh), ("hn2T", kc)], writes=[("pb", pbk)])
                    for g_ in range(2):
                        P.op("scalar", act(qpT[:, g_ * 4:(g_ + 1) * 4, :], pb[1 + g_][:].rearrange("p (a b) -> p a b", b=128), AF.Identity),
                             reads=[("pb", 1 + g_)], writes=[("qpT", g_)])
                    for hh in range(8):
                        pbk = 3 + hh // 2
                        P.op("tensor", mm(pb[pbk][:, (hh % 2) * 256:(hh % 2 + 1) * 256], qpT[:, hh, :], keysf[:, hh, :], True, True),
                             reads=[("qpT", hh // 4), "keysf"], writes=[("pb", pbk)])
                    for j in range(4):
                        P.op("scalar", act(sc[:, j * 512:(j + 1) * 512], pb[3 + j][:], AF.Identity),
                             reads=[("pb", 3 + j)], writes=["sc"])
                    psT = pb[7][:].bitcast(BF16)
                    for c in range(8):
                        P.op("tensor", tr(psT[:, c * 128:(c + 1) * 128], hn2T[:, c, :], identb[:]),
                             reads=[("hn2T", c), "identb"], writes=[("pb", 7)])
                    sc3 = sc.rearrange("p (g n) -> p g n", n=128)
                    sc23 = sc2.rearrange("p (g n) -> p g n", n=128)
                    for g_ in range(16):
                        P.op("vector", lambda e, g_=g_: e.max(out=v16[:, g_, 0:8], in_=sc3[:, g_, :]), reads=["sc"], writes=["v16"])
                        P.op("vector", lambda e, g_=g_: e.match_replace(out=sc23[:, g_, :], in_to_replace=v16[:, g_, 0:8],
                                                                      in_values=sc3[:, g_, :], imm_value=-1e30),
                             reads=["sc", "v16"], writes=["sc2"])
                        P.op("vector", lambda e, g_=g_: e.max(out=v16[:, g_, 8:16], in_=sc23[:, g_, :]), reads=["sc2", "v16"], writes=["v16"])
                        P.op("vector", lambda e, g_=g_: e.max_index(out=i16[:, g_, 0:8], in_max=v16[:, g_, 0:8], in_values=sc3[:, g_, :]),
                             reads=["sc", "v16"], writes=["i16"])
                        P.op("vector", lambda e, g_=g_: e.max_index(out=i16[:, g_, 8:16], in_max=v16[:, g_, 8:16], in_values=sc3[:, g_, :]),
                             reads=["sc", "v16", "i16"], writes=["i16"])
                    v4 = v16.rearrange("p (h two) k -> p h two k", two=2)
                    cand = tmpb.rearrange("p (h a b) -> p h a b", a=16, b=16)
                    P.op("vector", tt(cand, v4[:, :, 0, :].unsqueeze(3).to_broadcast([128, 8, 16, 16]),
                                      v4[:, :, 1, :].unsqueeze(2).to_broadcast([128, 8, 16, 16]), ALU.add),
                         reads=["v16"], writes=["tmpb"])
                    cand2 = tmpb.rearrange("p (h n) -> p h n", n=256)
                    sc2c = sc2.rearrange("p (h n) -> p h n", n=256)
                    pos3 = posu.rearrange("p (h k) -> p h k", k=16)
                    for hh in range(8):
                        P.op("vector", lambda e, hh=hh: e.max(out=best[:, hh, 0:8], in_=cand2[:, hh, :]), reads=["tmpb"], writes=["best"])
                        P.op("vector", lambda e, hh=hh: e.match_replace(out=sc2c[:, hh, :], in_to_replace=best[:, hh, 0:8],
                                                                      in_values=cand2[:, hh, :], imm_value=-1e30),
                             reads=["tmpb", "best"], writes=["sc2"])
                        P.op("vector", lambda e, hh=hh: e.max(out=best[:, hh, 8:16], in_=sc2c[:, hh, :]), reads=["sc2", "best"], writes=["best"])
                        P.op("vector", lambda e, hh=hh: e.max_index(out=pos3[:, hh, 0:8], in_max=best[:, hh, 0:8], in_values=cand2[:, hh, :]),
                             reads=["tmpb", "best"], writes=["posu"])
                        P.op("vector", lambda e, hh=hh: e.max_index(out=pos3[:, hh, 8:16], in_max=best[:, hh, 8:16], in_values=cand2[:, hh, :]),
                             reads=["tmpb", "best", "posu"], writes=["posu"])
                    P.op("vector", lambda e: e.tensor_single_scalar(out=k1u, in_=posu, scalar=4, op=ALU.logical_shift_right),
                         reads=["posu"], writes=["k1u"])
                    P.op("vector", lambda e: e.tensor_single_scalar(out=k2u, in_=posu, scalar=15, op=ALU.bitwise_and),
                         reads=["posu"], writes=["k2u"])
                    P.op("vector", cp(k1f.rearrange("p h k -> p (h k)"), k1u), reads=["k1u"], writes=["k1f"])
                    P.op("vector", cp(k2f.rearrange("p h k -> p (h k)"), k2u), reads=["k2u"], writes=["k2f"])
                    P.op("vector", cp(i16f, i16), reads=["i16"], writes=["i16f"])
                    i4 = i16f.rearrange("p (h two) k -> p h two k", two=2)
                    iob = iota16.unsqueeze(1).unsqueeze(1).to_broadcast([128, 8, 16, 16])
                    oh = sc.rearrange("p (h a b) -> p h a b", a=16, b=16)
                    for (kf, which, dst, nm) in ((k1f, 0, i1s, "i1s"), (k2f, 1, i2s, "i2s")):
                        P.op("vector", tt(oh, kf.unsqueeze(3).to_broadcast([128, 8, 16, 16]), iob, ALU.is_equal),
                             reads=["k1f", "k2f", "cst", "sc"], writes=["sc"])
                        P.op("vector", tt(oh, oh, i4[:, :, which, :].unsqueeze(2).to_broadcast([128, 8, 16, 16]), ALU.mult),
                             reads=["sc", "i16f"], writes=["sc"])
                        P.op("vector", red(dst, sc.rearrange("p (a b) -> p a b", b=16), ALU.add), reads=["sc"], writes=[nm])
                    P.op("vector", stt(idxf, i1s, 128.0, i2s, ALU.mult, ALU.add), reads=["i1s", "i2s"], writes=["idxf"])
                    P.op("vector", cp(idxu, idxf), reads=["idxf"], writes=["idxu"])
                    P.op("vector", tt(gt, best, best[:, :, 0:1].to_broadcast([128, 8, 16]), ALU.subtract), reads=["best"], writes=["gt"])
                    P.op("scalar", act(gt, gt, AF.Exp), reads=["gt"], writes=["gt"])
                    P.op("vector", red(gs, gt, ALU.add), reads=["gt"], writes=["gs"])
                    P.op("vector", rcp(gs, gs), reads=["gs"], writes=["gs"])
                    P.op("vector", tt(gt, gt, gs.unsqueeze(2).to_broadcast([128, 8, 16]), ALU.mult), reads=["gt", "gs"], writes=["gt"])
                    for j in range(128):
                        b_ = j % NG
                        P.dma("gpsimd", gather(Ug[b_], pub[li], idxu[:, j:j + 1]), reads=["idxu"], writes=[("Ug", b_)])
                        P.op("vector", lambda e, b_=b_, j=j: e.scalar_tensor_tensor(
                            out=junk, in0=psT, scalar=1.0, in1=Ug[b_], op0=ALU.mult, op1=ALU.mult,
                            accum_out=actv[:, j:j + 1]),
                            reads=[("pb", 7), ("Ug", b_)], writes=["junk", ("actv", j)])
                    actK = [("actv", j) for j in range(128)]
                    P.op("scalar", act(coef, actv, AF.Gelu_apprx_tanh), reads=actK, writes=["coef"])
                    P.op("vector", tt(coef, coef, gt.rearrange("p h k -> p (h k)"), ALU.mult), reads=["coef", "gt"], writes=["coef"])
                    for j in range(128):
                        b_ = j % NG
                        d_ = j % NDG
                        P.dma("gpsimd", gather(Vg[b_], pvb[li], idxu[:, j:j + 1]), reads=["idxu"], writes=[("Vg", b_)])
                        P.op("scalar", act(diag[d_], ident, AF.Identity, scale=coef[:, j:j + 1]),
                             reads=["coef", "cst"], writes=[("diag", d_)])
                        for hf in range(2):
                            P.op("tensor", mm(pb[1 + hf][:], diag[d_], Vg[b_][:, hf * 512:(hf + 1) * 512], j == 0, j == 127),
                                 reads=[("diag", d_), ("Vg", b_)], writes=[("pb", 1 + hf)])
                    for hf in range(2):
                        P.op("scalar", act(ytok[:, hf * 512:(hf + 1) * 512], pb[1 + hf][:], AF.Identity),
                             reads=[("pb", 1 + hf)], writes=[("ytok", hf)])
                    for c in range(8):
                        pbk = 3 + c % 2
                        P.op("tensor", tr(pb[pbk][:, 0:128], ytok[:, c * 128:(c + 1) * 128], ident),
                             reads=[("ytok", c // 4), "cst"], writes=[("pb", pbk)])
                        P.op("vector", stt(xT[:, c, t0:t0 + 128], pb[pbk][:, 0:128], mT[:, 40 + c, s:s + 1], xT[:, c, t0:t0 + 128],
                                           ALU.mult, ALU.add),
                             reads=[("pb", pbk), "mT", ("xT", c, t_)], writes=[("xT", c, t_)])
                P.barrier()

        if dbg is not None and DEBUG.get("dump") == "hT":
            dtmp = [av(ARB - 4096 + i * 2048, F32, 512) for i in range(2)]
            k_ = 0
            for c in range(NCH):
                for (s0, n) in BLOCKS:
                    P.op("scalar", act(dtmp[k_ % 2][:, :n], hT[:, c, s0:s0 + n], AF.Identity), reads=[], writes=[("dtmp", k_ % 2)])
                    P.dma("sync", dm(dbg[c * 128:(c + 1) * 128, s0:s0 + n], dtmp[k_ % 2][:, :n]), reads=[("dtmp", k_ % 2)], writes=[("dbg", c, s0)])
                    k_ += 1
        elif dbg is not None and DEBUG.get("dump") in ("zT", "oT"):
            src = zTd if DEBUG.get("dump") == "zT" else oTd
            dtmp = [av(ARB - 4096 + i * 2048, F32, 512) for i in range(2)]
            dtb = [av(ARB - 8192 + i * 1024, BF16, 512) for i in range(2)]
            k_ = 0
            for c in range(NCH):
                for (s0, n) in BLOCKS:
                    P.dma("sync", dm(dtb[k_ % 2][:, :n], src[c, :, s0:s0 + n]), reads=[], writes=[("dtb", k_ % 2)])
                    P.op("scalar", act(dtmp[k_ % 2][:, :n], dtb[k_ % 2][:, :n], AF.Identity), reads=[("dtb", k_ % 2)], writes=[("dtmp", k_ % 2)])
                    P.dma("sync", dm(dbg[c * 128:(c + 1) * 128, s0:s0 + n], dtmp[k_ % 2][:, :n]), reads=[("dtmp", k_ % 2)], writes=[("dbg", c, s0)])
                    k_ += 1
        elif dbg is not None:
            for c in range(NCH):
                P.dma("sync", dm(dbg[c * 128:(c + 1) * 128, :], xT[:, c, :]), reads=[], writes=[("dbg", c)])
        fg = gvt[:, 16:24]
        sq = [av(HB + i * 2048, F32, 512) for i in range(2)]
        r1 = av(HB + 4096, F32, 512)
        rstd = av(HB + 6144, F32, 512)
        ob_ = [av(HB + 8192 + i * 2048, F32, 512) for i in range(4)]
        for (bi, (s0, n)) in enumerate(BLOCKS):
            if bi == 0:
                continue
            for c in range(NCH):
                P.op("scalar", act(sq[c % 2][:, :n], xT[:, c, s0:s0 + n], AF.Square), reads=[], writes=[("fsq", c % 2)])
                P.op("tensor", mm(pb[bi % 2][:, :n], onesf[:], sq[c % 2][:, :n], c == 0, c == NCH - 1),
                     reads=[("fsq", c % 2)], writes=[("pb", bi % 2)])
            P.op("scalar", act(r1[:, :n], pb[bi % 2][:, :n], AF.Sqrt, scale=1.0 / D, bias=epst[:, 0:1]),
                 reads=[("pb", bi % 2)], writes=["fr1"])
            P.op("vector", rcp(rstd[:, :n], r1[:, :n]), reads=["fr1"], writes=["frstd"])
            for c in range(NCH):
                o_ = ob_[c % 4]
                P.op("vector", stt(o_[:, :n], xT[:, c, s0:s0 + n], fg[:, c:c + 1], rstd[:, :n], ALU.mult, ALU.mult),
                     reads=["frstd"], writes=[("fo", c % 4)])
                P.dma("sync", dm(outT[c * 128:(c + 1) * 128, s0 - CTX:s0 - CTX + n], o_[:, :n]),
                      reads=[("fo", c % 4)], writes=[("outT", c, bi)])
        P.emit()
    return nc


def _lay(W):
    K, N = W.shape
    return np.ascontiguousarray(W.reshape(8, 128, N // 128, 128).transpose(2, 1, 0, 3))


def _fm(v):
    return np.ascontiguousarray(np.asarray(v).reshape(8, 128).T)


def _rope_tables():
    p = np.arange(128)
    dm_ = p % 64
    axis = dm_ // 32
    half = (dm_ % 32) // 16
    f = dm_ % 16
    l = np.arange(LAT)
    pos = np.stack([l // 64, l % 64], 0).astype(np.float32)
    inv = (np.float32(10000.0) ** (-(np.arange(16, dtype=np.float32)) / np.float32(16))).astype(np.float32)
    ang = pos[axis, :] * inv[f][:, None]
    cosT = np.cos(ang).astype(np.float32)
    sinT = np.sin(ang).astype(np.float32)
    sinT = np.where(half[:, None] == 0, -sinT, sinT).astype(np.float32)
    return np.ascontiguousarray(np.stack([cosT, sinT], 0))


def _consts():
    ident = np.eye(128, dtype=np.float32)
    perm = np.zeros((128, 128), np.float32)
    for i in range(128):
        b16 = (i % 32) // 16
        j = i + 16 if b16 == 0 else i - 16
        perm[i, j] = 1.0
    iota = np.tile(np.arange(16, dtype=np.float32)[None, :], (128, 1))
    return np.ascontiguousarray(np.concatenate([ident, perm, iota], 1))


def prep_shared(inp):
    sh = {}
    f = lambda a: np.asarray(a, dtype=np.float32)
    pv = np.zeros((DEPTH, 128, NPV), np.float32)
    for li in range(DEPTH):
        pv[li, :, PV_N1:PV_N1 + 8] = _fm(inp["norm1_g"][li])
        pv[li, :, PV_N2:PV_N2 + 8] = _fm(inp["norm2_g"][li])
        for k in range(4):
            pv[li, :, PV_CW + k * 8:PV_CW + k * 8 + 8] = _fm(inp["conv_w"][li, k])
        pv[li, :, PV_CB:PV_CB + 8] = _fm(inp["conv_b"][li])
        for d_ in range(2):
            for g_ in range(2):
                o = PV_LB + (d_ * 2 + g_) * 8
                pv[li, :, o:o + 8] = _fm(inp["lru_b"][li, d_, g_])
            pv[li, :, PV_LL + d_ * 8:PV_LL + d_ * 8 + 8] = _fm(inp["lru_lam"][li, d_])
        for j in range(6):
            pv[li, :, PV_MB + j * 8:PV_MB + j * 8 + 8] = _fm(inp["mod_b"][li, j * 1024:(j + 1) * 1024])
        pv[li, :, PV_SG] = inp["subln_g"][li]
        pv[li, :, PV_DL:PV_DL + 256] = np.broadcast_to(f(inp["diff_lam"][li]).reshape(1, 256), (128, 256))
    sh["pv"] = pv
    sh["modw"] = np.stack([_lay(f(inp["mod_w"][li])) for li in range(DEPTH)])
    sh["win"] = np.stack([_lay(f(inp["w_in"][li])) for li in range(DEPTH)])
    sh["wbl"] = np.stack([_lay(f(inp["w_br_lru"][li])) for li in range(DEPTH)])
    sh["wba"] = np.stack([_lay(f(inp["w_br_attn"][li])) for li in range(DEPTH)])
    sh["wo"] = np.stack([_lay(f(inp["w_out"][li])) for li in range(DEPTH)])
    sh["wq"] = np.stack([_lay(f(inp["peer_wq"][li])) for li in range(DEPTH)])
    lw = f(inp["lru_w"])
    lruw = np.zeros((DEPTH, 8, 128, 4, 128), np.float32)
    for c in range(8):
        for d_ in range(2):
            for g_ in range(2):
                lruw[:, c, 0:64, d_ * 2 + g_, 0:64] = lw[:, d_, g_, 2 * c]
                lruw[:, c, 64:128, d_ * 2 + g_, 64:128] = lw[:, d_, g_, 2 * c + 1]
    sh["lruw"] = lruw
    pk = f(inp["peer_keys"])
    kb = np.zeros((DEPTH, 128, 8, 256), np.float32)
    for h in range(8):
        kb[:, 0:64, h, 0:128] = pk[:, h, 0].transpose(0, 2, 1)
        kb[:, 64:128, h, 128:256] = pk[:, h, 1].transpose(0, 2, 1)
    sh["keysbd"] = kb
    for li in range(DEPTH):
        sh["pu%d" % li] = np.ascontiguousarray(f(inp["peer_u"][li]))
        sh["pvt%d" % li] = np.ascontiguousarray(f(inp["peer_v"][li]))
    sh["cst"] = _consts()
    sh["ropet"] = _rope_tables()
    return sh


def prep_core(inp, b):
    xcat = np.concatenate([np.asarray(inp["ctx"][b], np.float32), np.asarray(inp["x"][b], np.float32)], 0)
    gv = np.concatenate([_fm(inp["c"][b]), _fm(inp["c_ctx"]), _fm(inp["final_g"])], 1).astype(np.float32)
    return {"xT0": np.ascontiguousarray(xcat.T), "gv": np.ascontiguousarray(gv)}


def kernel(**inputs):
    nb = inputs["x"].shape[0]
    sh = prep_shared(inputs)
    nc = build_program()
    in_maps = []
    for b in range(nb):
        m = dict(sh)
        m.update(prep_core(inputs, b))
        in_maps.append(m)
    res = run_bass_kernel_spmd(nc, in_maps, core_ids=list(range(nb)))
    out = np.stack([np.ascontiguousarray(res.results[b]["outT"].T) for b in range(nb)], 0)
    return out.astype(np.float32)
```

```python
import math
from contextlib import ExitStack

import numpy as np
import concourse.bass as bass
import concourse.mybir as mybir
from concourse.bass_utils import run_bass_kernel_spmd

F32 = mybir.dt.float32
BF16 = mybir.dt.bfloat16
U32 = mybir.dt.uint32
AF = mybir.ActivationFunctionType
ALU = mybir.AluOpType
AX = mybir.AxisListType

ENGS = ["sync", "scalar", "vector", "gpsimd", "tensor"]

D = 1024
NCH = 8
CTX = 256
LAT = 2048
T = CTX + LAT
NT = T // 128
DEPTH = 4
EPS = 1e-6
BLOCKS = [(0, 256)] + [(256 + 512 * i, 512) for i in range(4)]
NEXP = 16384

PV_N1 = 0
PV_N2 = 8
PV_CW = 16
PV_CB = 48
PV_LB = 56
PV_LL = 88
PV_MB = 104
PV_SG = 152
PV_DL = 153
NPV = 409

DEBUG = {}


class Prog:
    def __init__(self, nc, n_dma_slots=16):
        self.nc = nc
        self.ops = {e: [] for e in ENGS}
        self.cnt = {e: 0 for e in ENGS}
        self.last_w = {}
        self.readers = {}
        self.seen = {e: {} for e in ENGS}
        self.nslots = n_dma_slots
        self.dma_i = {e: 0 for e in ENGS}
        self.dma_last = {}

    def _deps(self, eng, reads, writes):
        deps = {}

        def add(tok):
            if tok is None:
                return
            k, v = tok
            if deps.get(k, 0) < v:
                deps[k] = v
        for r in reads:
            add(self.last_w.get(r))
        for w in writes:
            add(self.last_w.get(w))
            for t in self.readers.get(w, ()):
                add(t)
        waits = []
        for k, v in deps.items():
            if eng == "tensor" and k == ("eng", "tensor"):
                continue
            if self.seen[eng].get(k, 0) >= v:
                continue
            self.seen[eng][k] = v
            waits.append((k, v))
        return waits

    def _commit(self, tok, reads, writes):
        for w in writes:
            self.last_w[w] = tok
            self.readers[w] = []
        for r in reads:
            if r in writes:
                continue
            self.readers.setdefault(r, []).append(tok)

    def op(self, eng, fn, reads=(), writes=()):
        reads = list(reads)
        writes = list(writes)
        waits = self._deps(eng, reads, writes)
        self.cnt[eng] += 1
        tok = (("eng", eng), self.cnt[eng])
        self.ops[eng].append((fn, waits, ("eng", eng), 1))
        self._commit(tok, reads, writes)
        return tok

    def dma(self, eng, fn, reads=(), writes=()):
        reads = list(reads)
        writes = list(writes)
        i = self.dma_i[eng]
        self.dma_i[eng] += 1
        slot = i % self.nslots
        val = 16 * (i // self.nslots + 1)
        key = ("dma", eng, slot)
        waits = self._deps(eng, reads, writes)
        if val > 16:
            prev = val - 16
            if self.seen[eng].get(key, 0) < prev:
                self.seen[eng][key] = prev
                waits.append((key, prev))
        self.ops[eng].append((fn, waits, key, 16))
        tok = (key, val)
        self.dma_last[key] = val
        self._commit(tok, reads, writes)
        return tok

    def barrier(self):
        toks = [(("eng", e), self.cnt[e]) for e in ENGS if self.cnt[e] > 0]
        toks += list(self.dma_last.items())
        for e in ENGS:
            waits = []
            for k, v in toks:
                if self.seen[e].get(k, 0) >= v:
                    continue
                self.seen[e][k] = v
                waits.append((k, v))
            if waits:
                self.ops[e].append((None, waits, None, 0))

    def emit(self):
        nc = self.nc
        self.barrier()
        keys = set()
        for e in ENGS:
            for (_, waits, ik, _) in self.ops[e]:
                if ik is not None:
                    keys.add(ik)
                for k, _ in waits:
                    keys.add(k)
        with ExitStack() as st:
            sems = {}
            for k in sorted(keys, key=str):
                nm = "s_" + "_".join(str(x) for x in k)
                sems[k] = st.enter_context(nc.semaphore(nm))
            block = st.enter_context(nc.Block())

            def mk(e):
                def body(eng):
                    for (fn, waits, ik, inc) in self.ops[e]:
                        for k, v in waits:
                            eng.wait_ge(sems[k], v)
                        if fn is not None:
                            ins = fn(eng)
                            ins.then_inc(sems[ik], inc)
                return body
            for e in ENGS:
                getattr(block, e)(mk(e))


def mm(out, lhsT, rhs, start, stop):
    return lambda e: e.matmul(out, lhsT=lhsT, rhs=rhs, start=start, stop=stop)


def tr(out, in_, ident):
    return lambda e: e.transpose(out, in_, ident)


def act(out, in_, func, **kw):
    return lambda e: e.activation(out=out, in_=in_, func=func, **kw)


def tt(out, in0, in1, op):
    return lambda e: e.tensor_tensor(out=out, in0=in0, in1=in1, op=op)


def ts(out, in0, s1, s2, op0, op1=None):
    if op1 is None:
        return lambda e: e.tensor_scalar(out=out, in0=in0, scalar1=s1, scalar2=None, op0=op0)
    return lambda e: e.tensor_scalar(out=out, in0=in0, scalar1=s1, scalar2=s2, op0=op0, op1=op1)


def stt(out, in0, scalar, in1, op0, op1):
    return lambda e: e.scalar_tensor_tensor(out=out, in0=in0, scalar=scalar, in1=in1, op0=op0, op1=op1)


def cp(out, in_):
    return lambda e: e.tensor_copy(out=out, in_=in_)


def rcp(out, in_):
    return lambda e: e.reciprocal(out=out, in_=in_)


def red(out, in_, op):
    return lambda e: e.tensor_reduce(out=out, in_=in_, axis=AX.X, op=op)


def dm(out, in_):
    return lambda e: e.dma_start(out=out, in_=in_)


_BREG = {}


def gather(out, tab, idx):
    def f(e):
        if "r" not in _BREG:
            _BREG["r"] = e.to_reg(NEXP - 1)
        return e.indirect_dma_start(
            out=out, out_offset=None, in_=tab,
            in_offset=bass.IndirectOffsetOnAxis(ap=idx, axis=0),
            bounds_check=_BREG["r"], oob_is_err=False)
    return f


def build_program(n_layers=DEPTH, stop_phase=None, peer_tiles=None):
    nc = bass.Bass("TRN2", target_bir_lowering=False)
    _BREG.clear()

    def din(name, shape, dt=F32):
        return nc.dram_tensor(name, list(shape), dt, kind="ExternalInput").ap()

    xT0 = din("xT0", [D, T])
    gv = din("gv", [128, 24])
    pvd = din("pv", [DEPTH, 128, NPV])
    modw = din("modw", [DEPTH, 48, 128, 8, 128])
    win = din("win", [DEPTH, 56, 128, 8, 128])
    wbl = din("wbl", [DEPTH, 8, 128, 8, 128])
    wba = din("wba", [DEPTH, 8, 128, 8, 128])
    wo = din("wo", [DEPTH, 8, 128, 8, 128])
    wq = din("wq", [DEPTH, 8, 128, 8, 128])
    lruw = din("lruw", [DEPTH, 8, 128, 4, 128])
    keysbd = din("keysbd", [DEPTH, 128, 8, 256])
    ntab = 8 if DEBUG.get("no_peer_tables") else NEXP
    pu = [din("pu%d" % i, [ntab, D]) for i in range(DEPTH)]
    pvt_ = [din("pvt%d" % i, [ntab, D]) for i in range(DEPTH)]
    cst = din("cst", [128, 128 * 2 + 16])
    ropet = din("ropet", [2, 128, LAT])
    outT = nc.dram_tensor("outT", [D, LAT], F32, kind="ExternalOutput").ap()
    dbg = nc.dram_tensor("dbg", [D, T], F32, kind="ExternalOutput").ap() if DEBUG.get("dump") else None
    puv = [nc.dram_tensor("puv%d" % i, [NEXP, 2 * D], BF16, kind="Internal").ap() for i in range(DEPTH)]
    zTd = nc.dram_tensor("zTd", [8, 128, T], BF16, kind="Internal").ap()
    oTd = nc.dram_tensor("oTd", [8, 128, T], BF16, kind="Internal").ap()

    ARB = 110592
    with ExitStack() as st:
        def sb(name, shape, dt=F32):
            return st.enter_context(nc.sbuf_tensor(name, list(shape), dt))

        xT = sb("xT", [128, NCH, T])
        arena = sb("arena", [128, ARB // 4])
        cs_t = sb("cs_t", [128, 272])
        identb = sb("identb", [128, 128], BF16)
        permb = sb("permb", [128, 128], BF16)
        onesf = sb("onesf", [128, 128])
        onesb = sb("onesb", [128, 128], BF16)
        pvt = sb("pvt", [128, NPV])
        gvt = sb("gvt", [128, 24])
        sil = sb("sil", [128, 8, 2])
        mT = sb("mT", [128, 48, 2])
        A1 = sb("A1", [128, 8, 2])
        A2 = sb("A2", [128, 8, 2])
        sm = sb("sm", [128, 256])
        lamt = sb("lamt", [128, 4])
        csv = sb("csv", [128, 32])
        sgt = sb("sgt", [128, 1])
        NWB = 6
        wb = [sb("wb%d" % i, [128, 8, 128], BF16) for i in range(NWB)]
        lwb = [sb("lwb%d" % i, [128, 4, 128], BF16) for i in range(2)]
        pb = [st.enter_context(nc.psum_tensor("pb%d" % i, [128, 512], F32)) for i in range(8)]

        ident = cs_t[:, 0:128]
        permf = cs_t[:, 128:256]
        iota16 = cs_t[:, 256:272]

        def av(off, dt, n, shape=None):
            esz = 4 if dt in (F32, U32) else 2
            assert off % 4 == 0 and (n * esz) % 4 == 0 and off + n * esz <= ARB
            v = arena[:, off // 4: (off + n * esz) // 4]
            if dt != F32:
                v = v.bitcast(dt)
            if shape is not None:
                names = " ".join("d%d" % i for i in range(len(shape)))
                kw = {"d%d" % i: s for i, s in enumerate(shape)}
                v = v.rearrange("p (%s) -> p %s" % (names, names), **kw)
            return v

        P = Prog(nc)
        wbi = [0]

        def next_wb():
            i = wbi[0] % NWB
            wbi[0] += 1
            return i

        def load_w(src):
            i = next_wb()
            P.dma("gpsimd", dm(wb[i][:], src), reads=[], writes=[("wb", i)])
            return i

        P.dma("sync", dm(cs_t[:], cst), writes=["cst"])
        P.dma("sync", dm(gvt[:], gv), writes=["gvt"])
        for c in range(NCH):
            P.dma("sync", dm(xT[:, c, :], xT0[c * 128:(c + 1) * 128, :]),
                  writes=[("xT", c, b) for b in range(5)])
        P.op("vector", cp(identb[:], ident), reads=["cst"], writes=["identb"])
        P.op("vector", cp(permb[:], permf), reads=["cst"], writes=["permb"])
        P.op("gpsimd", lambda e: e.memset(onesf[:], 1.0), writes=["onesf"])
        P.op("gpsimd", lambda e: e.memset(onesb[:], 1.0), writes=["onesb"])
        P.op("scalar", act(sil[:, :, 0], gvt[:, 0:8], AF.Silu), reads=["gvt"], writes=["sil"])
        P.op("scalar", act(sil[:, :, 1], gvt[:, 8:16], AF.Silu), reads=["gvt", "sil"], writes=["sil"])
        if not DEBUG.get("no_peer_tables"):
            stage = [av(i * 16384, BF16, 4 * 2048, [4, 2048]) for i in range(2)]
            k_ = 0
            for li in range(n_layers):
                srcu = pu[li].rearrange("(c p i) d -> c p i d", p=128, i=4)
                srcv = pvt_[li].rearrange("(c p i) d -> c p i d", p=128, i=4)
                dstv = puv[li].rearrange("(c p i) d -> c p i d", p=128, i=4)
                for c in range(32):
                    b_ = k_ % 2
                    k_ += 1
                    for i in range(4):
                        P.dma("gpsimd", dm(stage[b_][:, i, 0:D], srcu[c, :, i, :]), writes=[("stg", b_, i, 0)])
                        P.dma("gpsimd", dm(stage[b_][:, i, D:2 * D], srcv[c, :, i, :]), writes=[("stg", b_, i, 1)])
                    P.dma("sync", dm(dstv[c], stage[b_][:]),
                          reads=[("stg", b_, i, h_) for i in range(4) for h_ in range(2)], writes=[("puv", li, c)])
        P.barrier()

        def norm_mod(blk_s0, n, s, Ax, shift_j, dst_fn, key_x, key_dst, sq_off, pbank, bw=512):
            sq = [av(sq_off + i * bw * 4, F32, bw) for i in range(2)]
            r1 = av(sq_off + 2 * bw * 4, F32, bw)
            rstd = av(sq_off + 3 * bw * 4, F32, bw)
            tmp = [av(sq_off + (4 + i) * bw * 4, F32, bw) for i in range(2)]
            for c in range(NCH):
                P.op("scalar", act(sq[c % 2][:, :n], xT[:, c, blk_s0:blk_s0 + n], AF.Square),
                     reads=[key_x(c)], writes=[("sq", c % 2)])
                P.op("tensor", mm(pb[pbank][:, :n], onesf[:], sq[c % 2][:, :n], c == 0, c == NCH - 1),
                     reads=[("sq", c % 2), "onesf"], writes=[("pb", pbank)])
            P.op("scalar", act(r1[:, :n], pb[pbank][:, :n], AF.Sqrt, scale=1.0 / D, bias=epst[:, 0:1]),
                 reads=[("pb", pbank), "epst"], writes=["r1"])
            P.op("vector", rcp(rstd[:, :n], r1[:, :n]), reads=["r1"], writes=["rstd"])
            for c in range(NCH):
                P.op("vector", tt(tmp[c % 2][:, :n], xT[:, c, blk_s0:blk_s0 + n], rstd[:, :n], ALU.mult),
                     reads=[key_x(c), "rstd"], writes=[("ntmp", c % 2)])
                P.op("scalar", act(dst_fn(c), tmp[c % 2][:, :n], AF.Identity,
                                   scale=Ax[:, c, s:s + 1], bias=mT[:, shift_j * 8 + c, s:s + 1]),
                     reads=[("ntmp", c % 2), "mT", "Ax"], writes=[key_dst(c)])

        epst = sb("epst", [128, 2])
        P.op("gpsimd", lambda e: e.memset(epst[:, 0:1], EPS), writes=["epst"])
        P.op("gpsimd", lambda e: e.memset(epst[:, 1:2], 1.0), writes=["epst"])

        hT = av(0, BF16, NCH * T, [NCH, T])
        HB = 36864

        for li in range(n_layers):
            last = (li == DEPTH - 1)
            lambda_init = 0.8 - 0.6 * math.exp(-0.3 * li)
            blocks = [(i, s0, n) for i, (s0, n) in enumerate(BLOCKS)]
            oblocks = [b for b in blocks if not (last and b[0] == 0)]

            P.dma("sync", dm(pvt[:], pvd[li]), writes=["pvt"])
            wst = [av(HB + i * 4096, F32, 1024, [8, 128]) for i in range(2)]
            psm = pb[0][:, 0:96]
            for jc in range(48):
                P.dma("sync", dm(wst[jc % 2], modw[li, jc]), writes=[("wst", jc % 2)])
                for kc in range(8):
                    P.op("tensor", mm(psm[:, jc * 2:(jc + 1) * 2], wst[jc % 2][:, kc, :], sil[:, kc, :], kc == 0, kc == 7),
                         reads=[("wst", jc % 2), "sil"], writes=[("pb", 0)])
            P.op("vector", tt(mT[:], psm.rearrange("p (a b) -> p a b", b=2),
                              pvt[:, PV_MB:PV_MB + 48].unsqueeze(2).to_broadcast([128, 48, 2]), ALU.add),
                 reads=[("pb", 0), "pvt"], writes=["mT"])
            for (Ax, j, pg, nm) in ((A1, 1, PV_N1, "A1"), (A2, 4, PV_N2, "A2")):
                P.op("vector", ts(Ax[:], mT[:, j * 8:(j + 1) * 8, :], 1.0, None, ALU.add), reads=["mT"], writes=[nm])
                P.op("vector", tt(Ax[:], Ax[:], pvt[:, pg:pg + 8].unsqueeze(2).to_broadcast([128, 8, 2]), ALU.mult),
                     reads=[nm, "pvt"], writes=[nm])
            dl = pvt[:, PV_DL:PV_DL + 256]
            P.op("vector", tt(sm[:, 0:64], dl[:, 0:64], dl[:, 64:128], ALU.mult), reads=["pvt"], writes=["sm"])
            P.op("vector", tt(sm[:, 64:128], dl[:, 128:192], dl[:, 192:256], ALU.mult), reads=["pvt", "sm"], writes=["sm"])
            P.op("vector", red(sm[:, 128:130], sm[:, 0:128].rearrange("p (a b) -> p a b", b=64), ALU.add),
                 reads=["sm"], writes=["sm"])
            P.op("scalar", act(sm[:, 130:132], sm[:, 128:130], AF.Exp), reads=["sm"], writes=["sm"])
            P.op("vector", ts(lamt[:, 0:1], sm[:, 130:131], sm[:, 131:132], float(lambda_init), ALU.subtract, ALU.add),
                 reads=["sm"], writes=["lamt"])
            P.op("vector", ts(lamt[:, 1:2], lamt[:, 0:1], -1.0, None, ALU.mult), reads=["lamt"], writes=["lamt"])
            P.op("vector", ts(sgt[:], pvt[:, PV_SG:PV_SG + 1], float(1.0 - lambda_init), None, ALU.mult),
                 reads=["pvt"], writes=["sgt"])
            P.op("scalar", act(sm[:, 136:152], pvt[:, PV_LL:PV_LL + 16], AF.Sigmoid), reads=["pvt", "sm"], writes=["sm"])
            P.op("scalar", act(sm[:, 136:152], sm[:, 136:152], AF.Ln), reads=["sm"], writes=["sm"])
            P.op("vector", ts(csv[:, 0:16], sm[:, 136:152], 8.0, None, ALU.mult), reads=["sm"], writes=["csv"])
            P.op("vector", ts(csv[:, 16:32], sm[:, 136:152], 16.0, None, ALU.mult), reads=["sm", "csv"], writes=["csv"])
            P.barrier()

            for (bi, s0, n) in blocks:
                s = 1 if bi == 0 else 0
                norm_mod(s0, n, s, A1, 0,
                         lambda c, s0=s0, n=n: hT[:, c, s0:s0 + n],
                         lambda c, bi=bi: ("xT", c, bi), lambda c, bi=bi: ("hT", c, bi),
                         HB, bi % 2)
            P.barrier()

            def inproj_fm(widx, bi, s0, n, pbank):
                for kc in range(8):
                    P.op("tensor", mm(pb[pbank][:, :n], wb[widx][:, kc, :], hT[:, kc, s0:s0 + n], kc == 0, kc == 7),
                         reads=[("wb", widx), ("hT", kc, bi)], writes=[("pb", pbank)])

            if stop_phase != "A" and not DEBUG.get("skip_b"):
                uc = av(HB, F32, T)
                Ab = av(HB + 9216, F32, T)
                Bb = av(HB + 18432, F32, T)
                C0 = av(HB + 27648, F32, T)
                C1 = av(HB + 36864, F32, T)
                ucb = av(HB + 46080, BF16, T)
                zc = av(HB + 50688, BF16, T)
                for c in range(NCH):
                    wu = load_w(win[li, 0 + c])
                    wg = load_w(win[li, 8 + c])
                    lw = c % 2
                    P.dma("gpsimd", dm(lwb[lw][:], lruw[li, c]), writes=[("lwb", lw)])
                    for (bi, s0, n) in blocks:
                        inproj_fm(wu, bi, s0, n, bi % 2)
                        P.op("scalar", act(C1[:, s0:s0 + n], pb[bi % 2][:, :n], AF.Identity),
                             reads=[("pb", bi % 2)], writes=[("C1", bi)])
                    allb = range(5)
                    cw = lambda k: pvt[:, PV_CW + k * 8 + c: PV_CW + k * 8 + c + 1]
                    for (a, b, bl) in ((0, CTX, [0]), (CTX, T, [1, 2, 3, 4])):
                        rk = [("C1", i) for i in bl]
                        wk_ = [("uc", i) for i in bl]
                        P.op("vector", ts(uc[:, a:b], C1[:, a:b], cw(2), pvt[:, PV_CB + c:PV_CB + c + 1], ALU.mult, ALU.add),
                             reads=rk + ["pvt"], writes=wk_)
                        P.op("vector", stt(uc[:, a + 2:b], C1[:, a:b - 2], cw(0), uc[:, a + 2:b], ALU.mult, ALU.add),
                             reads=rk + ["pvt"] + wk_, writes=wk_)
                        P.op("vector", stt(uc[:, a + 1:b], C1[:, a:b - 1], cw(1), uc[:, a + 1:b], ALU.mult, ALU.add),
                             reads=rk + ["pvt"] + wk_, writes=wk_)
                        P.op("vector", stt(uc[:, a:b - 1], C1[:, a + 1:b], cw(3), uc[:, a:b - 1], ALU.mult, ALU.add),
                             reads=rk + ["pvt"] + wk_, writes=wk_)
                    ucK = [("uc", i) for i in allb]
                    P.op("gpsimd", cp(ucb[:], uc[:]), reads=ucK, writes=["ucb"])
                    for d_ in range(2):
                        Cd = C0 if d_ == 0 else C1
                        CdK = [("C0" if d_ == 0 else "C1", i) for i in allb]
                        for (bi, s0, n) in blocks:
                            for g_, (dst, nm) in enumerate(((Ab, "Ab"), (Bb, "Bb"))):
                                pbk = 2 + g_ * 2 + (bi % 2)
                                P.op("tensor", mm(pb[pbk][:, :n], lwb[lw][:, d_ * 2 + g_, :], ucb[:, s0:s0 + n], True, True),
                                     reads=[("lwb", lw), "ucb"], writes=[("pb", pbk)])
                                bcol = PV_LB + (d_ * 2 + g_) * 8 + c
                                P.op("scalar", act(dst[:, s0:s0 + n], pb[pbk][:, :n], AF.Sigmoid, bias=pvt[:, bcol:bcol + 1]),
                                     reads=[("pb", pbk), "pvt"], writes=[(nm, bi)])
                        AbK = [("Ab", i) for i in allb]
                        BbK = [("Bb", i) for i in allb]
                        csc = csv[:, d_ * 8 + c: d_ * 8 + c + 1]
                        cs2c = csv[:, 16 + d_ * 8 + c: 16 + d_ * 8 + c + 1]
                        P.op("scalar", act(Cd[:], Ab[:], AF.Exp, scale=cs2c), reads=AbK + ["csv"], writes=CdK)
                        P.op("scalar", act(Ab[:], Ab[:], AF.Exp, scale=csc), reads=AbK + ["csv"], writes=AbK)
                        P.op("scalar", act(Cd[:], Cd[:], AF.Sqrt, scale=-1.0, bias=epst[:, 1:2]), reads=CdK + ["epst"], writes=CdK)
                        P.op("gpsimd", tt(Bb[:], Bb[:], uc[:], ALU.mult), reads=BbK + ucK, writes=BbK)
                        P.op("vector", tt(Bb[:], Bb[:], Cd[:], ALU.mult), reads=BbK + CdK, writes=BbK)
                        if d_ == 0:
                            P.op("vector", lambda e, Cd=Cd: e.tensor_tensor_scan(
                                out=Cd[:], data0=Ab[:], data1=Bb[:], initial=0.0, op0=ALU.mult, op1=ALU.add),
                                reads=AbK + BbK, writes=CdK)
                        else:
                            P.op("vector", lambda e, Cd=Cd: e.tensor_tensor_scan(
                                out=Cd[:, 0:CTX][:, ::-1], data0=Ab[:, 0:CTX][:, ::-1], data1=Bb[:, 0:CTX][:, ::-1],
                                initial=0.0, op0=ALU.mult, op1=ALU.add),
                                reads=AbK + BbK, writes=CdK)
                            P.op("vector", lambda e, Cd=Cd: e.tensor_tensor_scan(
                                out=Cd[:, CTX:T][:, ::-1], data0=Ab[:, CTX:T][:, ::-1], data1=Bb[:, CTX:T][:, ::-1],
                                initial=Cd[:, 0:1], op0=ALU.mult, op1=ALU.add),
                                reads=AbK + BbK + CdK, writes=CdK)
                    C0K = [("C0", i) for i in allb]
                    C1K = [("C1", i) for i in allb]
                    P.op("gpsimd", tt(C0[:], C0[:], C1[:], ALU.add), reads=C0K + C1K, writes=C0K)
                    for (bi, s0, n) in blocks:
                        inproj_fm(wg, bi, s0, n, bi % 2)
                        P.op("scalar", act(Ab[:, s0:s0 + n], pb[bi % 2][:, :n], AF.Gelu_apprx_tanh),
                             reads=[("pb", bi % 2)], writes=[("Ab", bi)])
                    P.op("vector", tt(zc[:], C0[:], Ab[:], ALU.mult), reads=C0K + [("Ab", i) for i in allb], writes=["zc"])
                    P.dma("sync", dm(zTd[c], zc[:]), reads=["zc"], writes=[("zTd", c)])
                P.barrier()

            if stop_phase not in ("A", "B"):
                cosT = av(HB, F32, LAT)
                sinT = av(HB + 8192, F32, LAT)
                qTt = av(HB + 16384, BF16, T)
                kTt = av(HB + 20992, BF16, T)
                vh = av(HB + 25600, BF16, NT * 128, [NT, 128])
                qraw = av(HB + 30208, BF16, 512)
                pT = [av(HB + 31232 + i * 1024, BF16, 512) for i in range(3)]
                wk = [av(HB + 34304 + i * 2048, F32, 512) for i in range(6)]
                oTc = av(HB + 46592, BF16, T)
                P.dma("sync", dm(cosT, ropet[0]), writes=["cosT"])
                P.dma("sync", dm(sinT, ropet[1]), writes=["sinT"])
                pti = [0]
                for h in range(8):
                    wqi = load_w(win[li, 16 + h])
                    wki = load_w(win[li, 24 + h])
                    wvi = load_w(win[li, 32 + h])
                    for (widx, dstT, nm) in ((wqi, qTt, "qT"), (wki, kTt, "kT")):
                        for (bi, s0, n) in (blocks if not DEBUG.get("skip_qk") else []):
                            if nm == "qT" and last and bi == 0:
                                continue
                            inproj_fm(widx, bi, s0, n, 7)
                            if bi == 0 or DEBUG.get("norope"):
                                P.op("scalar", act(dstT[:, s0:s0 + n], pb[7][:, :n], AF.Identity),
                                     reads=[("pb", 7)], writes=[(nm, bi)])
                            else:
                                l0 = s0 - CTX
                                P.op("scalar", act(qraw[:, :n], pb[7][:, :n], AF.Identity), reads=[("pb", 7)], writes=["qraw"])
                                P.op("tensor", mm(pb[6][:, :n], permb[:], qraw[:, :n], True, True),
                                     reads=["qraw", "permb"], writes=[("pb", 6)])
                                P.op("scalar", act(wk[2][:, :n], pb[7][:, :n], AF.Identity), reads=[("pb", 7)], writes=[("wk", 2)])
                                P.op("scalar", act(wk[3][:, :n], pb[6][:, :n], AF.Identity), reads=[("pb", 6)], writes=[("wk", 3)])
                                P.op("vector", tt(wk[0][:, :n], wk[2][:, :n], cosT[:, l0:l0 + n], ALU.mult),
                                     reads=[("wk", 2), "cosT"], writes=[("wk", 0)])
                                P.op("gpsimd", tt(wk[1][:, :n], wk[3][:, :n], sinT[:, l0:l0 + n], ALU.mult),
                                     reads=[("wk", 3), "sinT"], writes=[("wk", 1)])
                                P.op("vector", tt(dstT[:, s0:s0 + n], wk[0][:, :n], wk[1][:, :n], ALU.add),
                                     reads=[("wk", 0), ("wk", 1)], writes=[(nm, bi)])
                    for t_ in (range(NT) if not DEBUG.get("skip_v") else []):
                        bi = 0 if t_ < 2 else 1 + (t_ - 2) // 4
                        pbk = 6 + (t_ % 2)
                        for kc in range(8):
                            P.op("tensor", mm(pb[pbk][:, 0:128], hT[:, kc, t_ * 128:(t_ + 1) * 128], wb[wvi][:, kc, :], kc == 0, kc == 7),
                                 reads=[("wb", wvi), ("hT", kc, bi)], writes=[("pb", pbk)])
                        P.op("scalar", act(vh[:, t_, :], pb[pbk][:, 0:128], AF.Identity), reads=[("pb", pbk)], writes=[("vh", t_)])
                    for (bi, s0, n) in (oblocks if DEBUG.get("c_sub", 3) >= 2 else []):
                        ktiles = [0, 1] if bi == 0 else list(range(NT))
                        for m_ in range(2):
                            pso, psz = 2 + 2 * m_, 3 + 2 * m_
                            for ki, kt in enumerate(ktiles):
                                kbi = 0 if kt < 2 else 1 + (kt - 2) // 4
                                j = pti[0] % 2
                                j2 = pti[0] % 3
                                pti[0] += 1
                                P.op("tensor", mm(pb[j][:, :n], kTt[m_ * 64:(m_ + 1) * 64, kt * 128:(kt + 1) * 128],
                                                  qTt[m_ * 64:(m_ + 1) * 64, s0:s0 + n], True, True),
                                     reads=[("kT", kbi), ("qT", bi)], writes=[("pb", j)])
                                P.op("scalar", act(pT[j2][:, :n], pb[j][:, :n], AF.Exp, scale=0.125),
                                     reads=[("pb", j)], writes=[("pT", j2)])
                                P.op("tensor", mm(pb[pso][:, :n], vh[:, kt, :], pT[j2][:, :n], ki == 0, ki == len(ktiles) - 1),
                                     reads=[("vh", kt), ("pT", j2)], writes=[("pb", pso)])
                                P.op("tensor", mm(pb[psz][:, :n], onesb[:], pT[j2][:, :n], ki == 0, ki == len(ktiles) - 1),
                                     reads=["onesb", ("pT", j2)], writes=[("pb", psz)])
                        if DEBUG.get("c_sub", 3) < 3:
                            continue
                        W = lambda i: wk[i][:, :n]
                        P.op("vector", rcp(W(0), pb[3][:, :n]), reads=[("pb", 3)], writes=[("wk", 0)])
                        P.op("vector", tt(W(1), pb[2][:, :n], W(0), ALU.mult), reads=[("pb", 2), ("wk", 0)], writes=[("wk", 1)])
                        P.op("vector", rcp(W(2), pb[5][:, :n]), reads=[("pb", 5)], writes=[("wk", 2)])
                        P.op("vector", tt(W(3), pb[4][:, :n], W(2), ALU.mult), reads=[("pb", 4), ("wk", 2)], writes=[("wk", 3)])
                        P.op("vector", stt(W(4), W(3), lamt[:, 1:2], W(1), ALU.mult, ALU.add),
                             reads=[("wk", 3), ("wk", 1), "lamt"], writes=[("wk", 4)])
                        P.op("scalar", act(W(5), W(4), AF.Square), reads=[("wk", 4)], writes=[("wk", 5)])
                        P.op("tensor", mm(pb[6][:, :n], onesf[:], W(5), True, True), reads=["onesf", ("wk", 5)], writes=[("pb", 6)])
                        P.op("scalar", act(W(0), pb[6][:, :n], AF.Sqrt, scale=1.0 / 128, bias=epst[:, 0:1]),
                             reads=[("pb", 6), "epst"], writes=[("wk", 0)])
                        P.op("vector", rcp(W(2), W(0)), reads=[("wk", 0)], writes=[("wk", 2)])
                        P.op("vector", tt(W(1), W(4), W(2), ALU.mult), reads=[("wk", 4), ("wk", 2)], writes=[("wk", 1)])
                        P.op("scalar", act(oTc[:, s0:s0 + n], W(1), AF.Identity, scale=sgt[:, 0:1]),
                             reads=[("wk", 1), "sgt"], writes=[("oTc", bi)])
                    P.dma("sync", dm(oTd[h], oTc[:]), reads=[("oTc", i) for i in range(5)], writes=[("oTd", h)])
                P.barrier()

            if stop_phase not in ("A", "B", "C"):
                zb = av(HB, BF16, 8 * 512, [8, 512])
                ob = av(HB + 8192, BF16, 8 * 512, [8, 512])
                merged = av(HB + 16384, BF16, NCH * T, [NCH, T])
                wk = [av(HB + 53248 + i * 2048, F32, 512) for i in range(4)]
                for m_ in range(8):
                    wA = load_w(wbl[li, m_])
                    wB = load_w(wba[li, m_])
                    wGa = load_w(win[li, 40 + m_])
                    wGb = load_w(win[li, 48 + m_])
                    for (bi, s0, n) in oblocks:
                        P.dma("sync", dm(zb[:, :, :n], zTd[:, :, s0:s0 + n].rearrange("c p t -> p c t")),
                              reads=[("zTd", c) for c in range(8)], writes=["zb"])
                        P.dma("sync", dm(ob[:, :, :n], oTd[:, :, s0:s0 + n].rearrange("c p t -> p c t")),
                              reads=[("oTd", c) for c in range(8)], writes=["ob"])
                        for kc in range(8):
                            P.op("tensor", mm(pb[0][:, :n], wb[wA][:, kc, :], zb[:, kc, :n], kc == 0, kc == 7),
                                 reads=[("wb", wA), "zb"], writes=[("pb", 0)])
                        for kc in range(8):
                            P.op("tensor", mm(pb[1][:, :n], wb[wB][:, kc, :], ob[:, kc, :n], kc == 0, kc == 7),
                                 reads=[("wb", wB), "ob"], writes=[("pb", 1)])
                        inproj_fm(wGa, bi, s0, n, 2)
                        inproj_fm(wGb, bi, s0, n, 3)
                        W = lambda i: wk[i][:, :n]
                        P.op("scalar", act(W(0), pb[2][:, :n], AF.Sigmoid), reads=[("pb", 2)], writes=[("wk", 0)])
                        P.op("scalar", act(W(1), pb[3][:, :n], AF.Sigmoid), reads=[("pb", 3)], writes=[("wk", 1)])
                        P.op("vector", tt(W(2), W(0), pb[0][:, :n], ALU.mult), reads=[("pb", 0), ("wk", 0)], writes=[("wk", 2)])
                        P.op("vector", tt(W(3), W(1), pb[1][:, :n], ALU.mult), reads=[("pb", 1), ("wk", 1)], writes=[("wk", 3)])
                        P.op("gpsimd", tt(merged[:, m_, s0:s0 + n], W(2), W(3), ALU.add),
                             reads=[("wk", 2), ("wk", 3)], writes=[("mg", m_, bi)])
                for m_ in range(8):
                    wO = load_w(wo[li, m_])
                    for (bi, s0, n) in oblocks:
                        s = 1 if bi == 0 else 0
                        pbk = 4 + bi % 2
                        for kc in range(8):
                            P.op("tensor", mm(pb[pbk][:, :n], wb[wO][:, kc, :], merged[:, kc, s0:s0 + n], kc == 0, kc == 7),
                                 reads=[("wb", wO), ("mg", kc, bi)], writes=[("pb", pbk)])
                        P.op("vector", stt(xT[:, m_, s0:s0 + n], pb[pbk][:, :n], mT[:, 16 + m_, s:s + 1], xT[:, m_, s0:s0 + n],
                                           ALU.mult, ALU.add),
                             reads=[("pb", pbk), "mT", ("xT", m_, bi)], writes=[("xT", m_, bi)])
                P.barrier()

            if stop_phase not in ("A", "B", "C", "D"):
                wqb = av(0, BF16, 8 * 8 * 128, [8, 8, 128])
                keysf = av(16384, F32, 8 * 256, [8, 256])
                hn2T = av(24576, BF16, 8 * 128, [8, 128])
                qpT = av(26624, F32, 8 * 128, [8, 128])
                ytok = av(30720, F32, 1024)
                sc = av(34816, F32, 2048)
                sc2 = av(43008, F32, 2048)
                tmpb = av(51200, F32, 2048)
                SO = 59392
                v16 = av(SO, F32, 256, [16, 16])
                i16 = av(SO + 1024, U32, 256, [16, 16])
                i16f = av(SO + 2048, F32, 256, [16, 16])
                best = av(SO + 3072, F32, 128, [8, 16])
                posu = av(SO + 3584, U32, 128)
                k1u = av(SO + 4096, U32, 128)
                k2u = av(SO + 4608, U32, 128)
                k1f = av(SO + 5120, F32, 128, [8, 16])
                k2f = av(SO + 5632, F32, 128, [8, 16])
                i1s = av(SO + 6144, F32, 128)
                i2s = av(SO + 6656, F32, 128)
                idxf = av(SO + 7168, F32, 128)
                idxu = av(SO + 7680, U32, 128)
                gt = av(SO + 8192, F32, 128, [8, 16])
                gs = av(SO + 8704, F32, 8)
                actv = av(SO + 8768, F32, 128)
                coef = av(SO + 9280, F32, 128)
                NDG = 4
                diag = [av(SO + 9792 + i * 256, BF16, 128) for i in range(NDG)]
                junk = av(SO + 10816, BF16, 1024)
                GB = SO + 12864
                NG = 8
                UVg = [av(GB + i * 4096, BF16, 2048) for i in range(NG)]
                cj = av(108096 + 1024, F32, 128)
                tj = av(108096 + 1536, F32, 128)
                sqo = 24576
                for m_ in range(8):
                    P.dma("gpsimd", dm(wqb[:, m_, :, :], wq[li, m_]), writes=[("wqb", m_)])
                P.dma("sync", dm(keysf, keysbd[li]), writes=["keysf"])
                tiles = list(range(NT))
                if last:
                    tiles = tiles[2:]
                if peer_tiles is not None:
                    tiles = [t_ for t_ in tiles if t_ in peer_tiles]
                idxu_b = [idxu, av(108096, U32, 128)]
                gt_b = [gt, av(108608, F32, 128, [8, 16])]

                def peer_front(t_, par):
                    s = 1 if t_ < 2 else 0
                    t0 = t_ * 128
                    idxu_ = idxu_b[par]
                    gt_ = gt_b[par]
                    norm_mod(t0, 128, s, A2, 3,
                             lambda c: hn2T[:, c, :],
                             lambda c, t_=t_: ("xT", c, t_), lambda c: ("hn2T", c),
                             105024, 0, bw=128)
                    yield
                    for hh in range(8):
                        pbk = 1 + hh // 4
                        for kc in range(8):
                            P.op("tensor", mm(pb[pbk][:, (hh % 4) * 128:(hh % 4 + 1) * 128], wqb[:, hh, kc, :], hn2T[:, kc, :], kc == 0, kc == 7),
                                 reads=[("wqb", hh), ("hn2T", kc)], writes=[("pb", pbk)])
                    for g_ in range(2):
                        P.op("scalar", act(qpT[:, g_ * 4:(g_ + 1) * 4, :], pb[1 + g_][:].rearrange("p (a b) -> p a b", b=128), AF.Identity),
                             reads=[("pb", 1 + g_)], writes=[("qpT", g_)])
                    yield
                    psT = pb[3 + par][:].bitcast(BF16)
                    for c in range(8):
                        P.op("tensor", tr(psT[:, c * 128:(c + 1) * 128], hn2T[:, c, :], identb[:]),
                             reads=[("hn2T", c), "identb"], writes=[("pb", 3 + par)])
                    yield
                    for hh in range(8):
                        pbk = 1 + (hh // 2) % 2
                        P.op("tensor", mm(pb[pbk][:, (hh % 2) * 256:(hh % 2 + 1) * 256], qpT[:, hh, :], keysf[:, hh, :], True, True),
                             reads=[("qpT", hh // 4), "keysf"], writes=[("pb", pbk)])
                        if hh % 2 == 1:
                            j = hh // 2
                            P.op("scalar", act(sc[:, j * 512:(j + 1) * 512], pb[pbk][:], AF.Identity),
                                 reads=[("pb", pbk)], writes=["sc"])
                    yield
                    sc3 = sc.rearrange("p (g n) -> p g n", n=128)
                    sc23 = sc2.rearrange("p (g n) -> p g n", n=128)
                    for g_ in range(16):
                        P.op("vector", lambda e, g_=g_: e.max(out=v16[:, g_, 0:8], in_=sc3[:, g_, :]), reads=["sc"], writes=["v16"])
                        P.op("vector", lambda e, g_=g_: e.match_replace(out=sc23[:, g_, :], in_to_replace=v16[:, g_, 0:8],
                                                                      in_values=sc3[:, g_, :], imm_value=-1e30),
                             reads=["sc", "v16"], writes=["sc2"])
                        P.op("vector", lambda e, g_=g_: e.max(out=v16[:, g_, 8:16], in_=sc23[:, g_, :]), reads=["sc2", "v16"], writes=["v16"])
                        P.op("vector", lambda e, g_=g_: e.max_index(out=i16[:, g_, 0:8], in_max=v16[:, g_, 0:8], in_values=sc3[:, g_, :]),
                             reads=["sc", "v16"], writes=["i16"])
                        P.op("vector", lambda e, g_=g_: e.max_index(out=i16[:, g_, 8:16], in_max=v16[:, g_, 8:16], in_values=sc3[:, g_, :]),
                             reads=["sc", "v16", "i16"], writes=["i16"])
                        yield
                    v4 = v16.rearrange("p (h two) k -> p h two k", two=2)
                    cand = tmpb.rearrange("p (h a b) -> p h a b", a=16, b=16)
                    P.op("vector", tt(cand, v4[:, :, 0, :].unsqueeze(3).to_broadcast([128, 8, 16, 16]),
                                      v4[:, :, 1, :].unsqueeze(2).to_broadcast([128, 8, 16, 16]), ALU.add),
                         reads=["v16"], writes=["tmpb"])
                    yield
                    cand2 = tmpb.rearrange("p (h n) -> p h n", n=256)
                    sc2c = sc2.rearrange("p (h n) -> p h n", n=256)
                    pos3 = posu.rearrange("p (h k) -> p h k", k=16)
                    for hh in range(8):
                        P.op("vector", lambda e, hh=hh: e.max(out=best[:, hh, 0:8], in_=cand2[:, hh, :]), reads=["tmpb"], writes=["best"])
                        P.op("vector", lambda e, hh=hh: e.match_replace(out=sc2c[:, hh, :], in_to_replace=best[:, hh, 0:8],
                                                                      in_values=cand2[:, hh, :], imm_value=-1e30),
                             reads=["tmpb", "best"], writes=["sc2"])
                        P.op("vector", lambda e, hh=hh: e.max(out=best[:, hh, 8:16], in_=sc2c[:, hh, :]), reads=["sc2", "best"], writes=["best"])
                        P.op("vector", lambda e, hh=hh: e.max_index(out=pos3[:, hh, 0:8], in_max=best[:, hh, 0:8], in_values=cand2[:, hh, :]),
                             reads=["tmpb", "best"], writes=["posu"])
                        P.op("vector", lambda e, hh=hh: e.max_index(out=pos3[:, hh, 8:16], in_max=best[:, hh, 8:16], in_values=cand2[:, hh, :]),
                             reads=["tmpb", "best", "posu"], writes=["posu"])
                        yield
                    P.op("vector", lambda e: e.tensor_single_scalar(out=k1u, in_=posu, scalar=4, op=ALU.logical_shift_right),
                         reads=["posu"], writes=["k1u"])
                    P.op("vector", lambda e: e.tensor_single_scalar(out=k2u, in_=posu, scalar=15, op=ALU.bitwise_and),
                         reads=["posu"], writes=["k2u"])
                    P.op("vector", cp(k1f.rearrange("p h k -> p (h k)"), k1u), reads=["k1u"], writes=["k1f"])
                    P.op("vector", cp(k2f.rearrange("p h k -> p (h k)"), k2u), reads=["k2u"], writes=["k2f"])
                    P.op("vector", cp(i16f, i16), reads=["i16"], writes=["i16f"])
                    yield
                    i4 = i16f.rearrange("p (h two) k -> p h two k", two=2)
                    iob = iota16.unsqueeze(1).unsqueeze(1).to_broadcast([128, 8, 16, 16])
                    oh = sc.rearrange("p (h a b) -> p h a b", a=16, b=16)
                    for (kf, which, dst, nm) in ((k1f, 0, i1s, "i1s"), (k2f, 1, i2s, "i2s")):
                        P.op("vector", tt(oh, kf.unsqueeze(3).to_broadcast([128, 8, 16, 16]), iob, ALU.is_equal),
                             reads=["k1f", "k2f", "cst", "sc"], writes=["sc"])
                        yield
                        P.op("vector", tt(oh, oh, i4[:, :, which, :].unsqueeze(2).to_broadcast([128, 8, 16, 16]), ALU.mult),
                             reads=["sc", "i16f"], writes=["sc"])
                        yield
                        P.op("vector", red(dst, sc.rearrange("p (a b) -> p a b", b=16), ALU.add), reads=["sc"], writes=[nm])
                        yield
                    P.op("vector", stt(idxf, i1s, 128.0, i2s, ALU.mult, ALU.add), reads=["i1s", "i2s"], writes=["idxf"])
                    P.op("vector", cp(idxu_, idxf), reads=["idxf"], writes=[("idxu", par)])
                    P.op("vector", tt(gt_, best, best[:, :, 0:1].to_broadcast([128, 8, 16]), ALU.subtract), reads=["best"], writes=[("gt", par)])
                    P.op("scalar", act(gt_, gt_, AF.Exp), reads=[("gt", par)], writes=[("gt", par)])
                    P.op("vector", red(gs, gt_, ALU.add), reads=[("gt", par)], writes=["gs"])
                    P.op("vector", rcp(gs, gs), reads=["gs"], writes=["gs"])
                    P.op("vector", tt(gt_, gt_, gs.unsqueeze(2).to_broadcast([128, 8, 16]), ALU.mult), reads=[("gt", par), "gs"], writes=[("gt", par)])
                    yield

                def peer_back(t_, par):
                    s = 1 if t_ < 2 else 0
                    t0 = t_ * 128
                    idxu_ = idxu_b[par]
                    gt_ = gt_b[par]
                    psT = pb[3 + par][:].bitcast(BF16)
                    gflat = gt_.rearrange("p h k -> p (h k)")
                    for j in range(128):
                        b_ = j % NG
                        d_ = j % NDG
                        P.dma("gpsimd", gather(UVg[b_], puv[li], idxu_[:, j:j + 1]), reads=[("idxu", par)], writes=[("UVg", b_)])
                        P.op("vector", lambda e, b_=b_, j=j: e.scalar_tensor_tensor(
                            out=junk, in0=psT, scalar=1.0, in1=UVg[b_][:, 0:D], op0=ALU.mult, op1=ALU.mult,
                            accum_out=actv[:, j:j + 1]),
                            reads=[("pb", 3 + par), ("UVg", b_)], writes=["junk", ("actv", j)])
                        P.op("scalar", act(tj[:, j:j + 1], actv[:, j:j + 1], AF.Gelu_apprx_tanh), reads=[("actv", j)], writes=[("tj", j)])
                        P.op("scalar", act(cj[:, j:j + 1], tj[:, j:j + 1], AF.Identity, scale=gflat[:, j:j + 1]),
                             reads=[("tj", j), ("gt", par)], writes=[("cj", j)])
                        P.op("scalar", act(diag[d_], ident, AF.Identity, scale=cj[:, j:j + 1]),
                             reads=[("cj", j), "cst"], writes=[("diag", d_)])
                        for hf in range(2):
                            P.op("tensor", mm(pb[5 + hf][:], diag[d_], UVg[b_][:, D + hf * 512:D + (hf + 1) * 512], j == 0, j == 127),
                                 reads=[("diag", d_), ("UVg", b_)], writes=[("pb", 5 + hf)])
                        yield
                    for hf in range(2):
                        P.op("scalar", act(ytok[:, hf * 512:(hf + 1) * 512], pb[5 + hf][:], AF.Identity),
                             reads=[("pb", 5 + hf)], writes=[("ytok", hf)])
                    for c in range(8):
                        P.op("tensor", tr(pb[7][:, 0:128], ytok[:, c * 128:(c + 1) * 128], ident),
                             reads=[("ytok", c // 4), "cst"], writes=[("pb", 7)])
                        P.op("vector", stt(xT[:, c, t0:t0 + 128], pb[7][:, 0:128], mT[:, 40 + c, s:s + 1], xT[:, c, t0:t0 + 128],
                                           ALU.mult, ALU.add),
                             reads=[("pb", 7), "mT", ("xT", c, t_)], writes=[("xT", c, t_)])
                        yield

                def drive(gens):
                    gens = [g_ for g_ in gens if g_ is not None]
                    while gens:
                        for g_ in list(gens):
                            try:
                                next(g_)
                            except StopIteration:
                                gens.remove(g_)

                if tiles:
                    drive([peer_front(tiles[0], 0)])
                for i_, t_ in enumerate(tiles):
                    nxt = peer_front(tiles[i_ + 1], (i_ + 1) % 2) if (i_ + 1 < len(tiles) and not DEBUG.get("no_overlap")) else None
                    drive([peer_back(t_, i_ % 2), nxt])
                    if DEBUG.get("no_overlap") and i_ + 1 < len(tiles):
                        drive([peer_front(tiles[i_ + 1], (i_ + 1) % 2)])
                P.barrier()

        if dbg is not None and DEBUG.get("dump") == "hT":
            dtmp = [av(ARB - 4096 + i * 2048, F32, 512) for i in range(2)]
            k_ = 0
            for c in range(NCH):
                for (s0, n) in BLOCKS:
                    P.op("scalar", act(dtmp[k_ % 2][:, :n], hT[:, c, s0:s0 + n], AF.Identity), reads=[], writes=[("dtmp", k_ % 2)])
                    P.dma("sync", dm(dbg[c * 128:(c + 1) * 128, s0:s0 + n], dtmp[k_ % 2][:, :n]), reads=[("dtmp", k_ % 2)], writes=[("dbg", c, s0)])
                    k_ += 1
        elif dbg is not None and DEBUG.get("dump") in ("zT", "oT"):
            src = zTd if DEBUG.get("dump") == "zT" else oTd
            dtmp = [av(ARB - 4096 + i * 2048, F32, 512) for i in range(2)]
            dtb = [av(ARB - 8192 + i * 1024, BF16, 512) for i in range(2)]
            k_ = 0
            for c in range(NCH):
                for (s0, n) in BLOCKS:
                    P.dma("sync", dm(dtb[k_ % 2][:, :n], src[c, :, s0:s0 + n]), reads=[], writes=[("dtb", k_ % 2)])
                    P.op("scalar", act(dtmp[k_ % 2][:, :n], dtb[k_ % 2][:, :n], AF.Identity), reads=[("dtb", k_ % 2)], writes=[("dtmp", k_ % 2)])
                    P.dma("sync", dm(dbg[c * 128:(c + 1) * 128, s0:s0 + n], dtmp[k_ % 2][:, :n]), reads=[("dtmp", k_ % 2)], writes=[("dbg", c, s0)])
                    k_ += 1
        elif dbg is not None:
            for c in range(NCH):
                P.dma("sync", dm(dbg[c * 128:(c + 1) * 128, :], xT[:, c, :]), reads=[], writes=[("dbg", c)])
        fg = gvt[:, 16:24]
        sq = [av(HB + i * 2048, F32, 512) for i in range(2)]
        r1 = av(HB + 4096, F32, 512)
        rstd = av(HB + 6144, F32, 512)
        ob_ = [av(HB + 8192 + i * 2048, F32, 512) for i in range(4)]
        for (bi, (s0, n)) in enumerate(BLOCKS):
            if bi == 0:
                continue
            for c in range(NCH):
                P.op("scalar", act(sq[c % 2][:, :n], xT[:, c, s0:s0 + n], AF.Square), reads=[], writes=[("fsq", c % 2)])
                P.op("tensor", mm(pb[bi % 2][:, :n], onesf[:], sq[c % 2][:, :n], c == 0, c == NCH - 1),
                     reads=[("fsq", c % 2)], writes=[("pb", bi % 2)])
            P.op("scalar", act(r1[:, :n], pb[bi % 2][:, :n], AF.Sqrt, scale=1.0 / D, bias=epst[:, 0:1]),
                 reads=[("pb", bi % 2)], writes=["fr1"])
            P.op("vector", rcp(rstd[:, :n], r1[:, :n]), reads=["fr1"], writes=["frstd"])
            for c in range(NCH):
                o_ = ob_[c % 4]
                P.op("vector", stt(o_[:, :n], xT[:, c, s0:s0 + n], fg[:, c:c + 1], rstd[:, :n], ALU.mult, ALU.mult),
                     reads=["frstd"], writes=[("fo", c % 4)])
                P.dma("sync", dm(outT[c * 128:(c + 1) * 128, s0 - CTX:s0 - CTX + n], o_[:, :n]),
                      reads=[("fo", c % 4)], writes=[("outT", c, bi)])
        P.emit()
    return nc


def _lay(W):
    K, N = W.shape
    return np.ascontiguousarray(W.reshape(8, 128, N // 128, 128).transpose(2, 1, 0, 3))


def _fm(v):
    return np.ascontiguousarray(np.asarray(v).reshape(8, 128).T)


def _rope_tables():
    p = np.arange(128)
    dm_ = p % 64
    axis = dm_ // 32
    half = (dm_ % 32) // 16
    f = dm_ % 16
    l = np.arange(LAT)
    pos = np.stack([l // 64, l % 64], 0).astype(np.float32)
    inv = (np.float32(10000.0) ** (-(np.arange(16, dtype=np.float32)) / np.float32(16))).astype(np.float32)
    ang = pos[axis, :] * inv[f][:, None]
    cosT = np.cos(ang).astype(np.float32)
    sinT = np.sin(ang).astype(np.float32)
    sinT = np.where(half[:, None] == 0, -sinT, sinT).astype(np.float32)
    return np.ascontiguousarray(np.stack([cosT, sinT], 0))


def _consts():
    ident = np.eye(128, dtype=np.float32)
    perm = np.zeros((128, 128), np.float32)
    for i in range(128):
        b16 = (i % 32) // 16
        j = i + 16 if b16 == 0 else i - 16
        perm[i, j] = 1.0
    iota = np.tile(np.arange(16, dtype=np.float32)[None, :], (128, 1))
    return np.ascontiguousarray(np.concatenate([ident, perm, iota], 1))


def prep_shared(inp):
    sh = {}
    f = lambda a: np.asarray(a, dtype=np.float32)
    pv = np.zeros((DEPTH, 128, NPV), np.float32)
    for li in range(DEPTH):
        pv[li, :, PV_N1:PV_N1 + 8] = _fm(inp["norm1_g"][li])
        pv[li, :, PV_N2:PV_N2 + 8] = _fm(inp["norm2_g"][li])
        for k in range(4):
            pv[li, :, PV_CW + k * 8:PV_CW + k * 8 + 8] = _fm(inp["conv_w"][li, k])
        pv[li, :, PV_CB:PV_CB + 8] = _fm(inp["conv_b"][li])
        for d_ in range(2):
            for g_ in range(2):
                o = PV_LB + (d_ * 2 + g_) * 8
                pv[li, :, o:o + 8] = _fm(inp["lru_b"][li, d_, g_])
            pv[li, :, PV_LL + d_ * 8:PV_LL + d_ * 8 + 8] = _fm(inp["lru_lam"][li, d_])
        for j in range(6):
            pv[li, :, PV_MB + j * 8:PV_MB + j * 8 + 8] = _fm(inp["mod_b"][li, j * 1024:(j + 1) * 1024])
        pv[li, :, PV_SG] = inp["subln_g"][li]
        pv[li, :, PV_DL:PV_DL + 256] = np.broadcast_to(f(inp["diff_lam"][li]).reshape(1, 256), (128, 256))
    sh["pv"] = pv
    sh["modw"] = np.stack([_lay(f(inp["mod_w"][li])) for li in range(DEPTH)])
    sh["win"] = np.stack([_lay(f(inp["w_in"][li])) for li in range(DEPTH)])
    sh["wbl"] = np.stack([_lay(f(inp["w_br_lru"][li])) for li in range(DEPTH)])
    sh["wba"] = np.stack([_lay(f(inp["w_br_attn"][li])) for li in range(DEPTH)])
    sh["wo"] = np.stack([_lay(f(inp["w_out"][li])) for li in range(DEPTH)])
    sh["wq"] = np.stack([_lay(f(inp["peer_wq"][li])) for li in range(DEPTH)])
    lw = f(inp["lru_w"])
    lruw = np.zeros((DEPTH, 8, 128, 4, 128), np.float32)
    for c in range(8):
        for d_ in range(2):
            for g_ in range(2):
                lruw[:, c, 0:64, d_ * 2 + g_, 0:64] = lw[:, d_, g_, 2 * c]
                lruw[:, c, 64:128, d_ * 2 + g_, 64:128] = lw[:, d_, g_, 2 * c + 1]
    sh["lruw"] = lruw
    pk = f(inp["peer_keys"])
    kb = np.zeros((DEPTH, 128, 8, 256), np.float32)
    for h in range(8):
        kb[:, 0:64, h, 0:128] = pk[:, h, 0].transpose(0, 2, 1)
        kb[:, 64:128, h, 128:256] = pk[:, h, 1].transpose(0, 2, 1)
    sh["keysbd"] = kb
    for li in range(DEPTH):
        sh["pu%d" % li] = np.ascontiguousarray(f(inp["peer_u"][li]))
        sh["pvt%d" % li] = np.ascontiguousarray(f(inp["peer_v"][li]))
    sh["cst"] = _consts()
    sh["ropet"] = _rope_tables()
    return sh


def prep_core(inp, b):
    xcat = np.concatenate([np.asarray(inp["ctx"][b], np.float32), np.asarray(inp["x"][b], np.float32)], 0)
    gv = np.concatenate([_fm(inp["c"][b]), _fm(inp["c_ctx"]), _fm(inp["final_g"])], 1).astype(np.float32)
    return {"xT0": np.ascontiguousarray(xcat.T), "gv": np.ascontiguousarray(gv)}


def kernel(**inputs):
    nb = inputs["x"].shape[0]
    sh = prep_shared(inputs)
    nc = build_program()
    in_maps = []
    for b in range(nb):
        m = dict(sh)
        m.update(prep_core(inputs, b))
        in_maps.append(m)
    res = run_bass_kernel_spmd(nc, in_maps, core_ids=list(range(nb)))
    out = np.stack([np.ascontiguousarray(res.results[b]["outT"].T) for b in range(nb)], 0)
    return out.astype(np.float32)
```
